# Optimizing a Trainium2 kernel written in Bass

```python
import jax, jax.numpy as jnp
from jax import lax
import numpy as np

D_MODEL = 1024
BATCH = 8
SEQ = 2048
DEPTH = 2
DEC_BATCH = 128
DEC_SEQ = 4
PAST_LEN = 16384
PAGE_SIZE = 128

N_MIXERS = 2
POOL_WINDOWS = (2, 4, 8, 16)
POOL_GROUPS = len(POOL_WINDOWS)
POOL_GROUP_DIM = D_MODEL // POOL_GROUPS
POOL_CTX = max(POOL_WINDOWS) - 1
N_HEADS = 8
HEAD_K = 128
HEAD_V = D_MODEL // N_HEADS
F_DIM = N_HEADS * HEAD_K
V_DIM = N_HEADS * HEAD_V
GLA_CHUNK = 64
D_FF = 2816
CONV_W = 3
N_POOL = (DEPTH + 1) // 2
N_HGRN = DEPTH // 2
EPS = 1e-6

kernel_name = "hybrid_pool_hgrn2_convffn_step"


def rmsnorm(x, g):
    xf = x.astype(jnp.float32)
    r = lax.rsqrt(jnp.mean(xf * xf, axis=-1, keepdims=True) + EPS)
    return (xf * r * g.astype(jnp.float32)).astype(x.dtype)


def pool_mixer(h, ctx, pos0, w_pool, scale):
    B, L, D = h.shape
    full = jnp.concatenate([ctx.astype(h.dtype), h], axis=1)
    csum = jnp.cumsum(full.astype(jnp.float32), axis=1)
    csum = jnp.concatenate([jnp.zeros((B, 1, D), jnp.float32), csum], axis=1)
    hi = POOL_CTX + 1 + np.arange(L)
    hf = h.astype(jnp.float32)
    outs = []
    for g, w in enumerate(POOL_WINDOWS):
        lo_c, hi_c = g * POOL_GROUP_DIM, (g + 1) * POOL_GROUP_DIM
        cnt = np.minimum(pos0 + np.arange(L) + 1, w).astype(np.float32)
        s = csum[:, hi, lo_c:hi_c] - csum[:, hi - w, lo_c:hi_c]
        outs.append(s / cnt[None, :, None] - hf[..., lo_c:hi_c])
    p = jnp.stack(outs, axis=2)
    y = jnp.einsum('blgd,gde->blge', p, w_pool.astype(jnp.float32)).reshape(B, L, D)
    y = y * scale.astype(jnp.float32)
    return y.astype(h.dtype), full[:, -POOL_CTX:]


def gla_chunked(q, k, v, log_f, s0, chunk):
    B, L, H, DK = q.shape
    DV = v.shape[-1]
    n = -(-L // chunk)
    pad = n * chunk - L

    def prep(a):
        a = jnp.pad(a, ((0, 0), (0, pad), (0, 0), (0, 0)))
        return a.reshape(B, n, chunk, H, a.shape[-1]).transpose(1, 0, 3, 2, 4)

    qs, ks, vs, gs = prep(q), prep(k), prep(v), prep(log_f)
    causal = jnp.tril(jnp.ones((chunk, chunk), bool))[:, :, None]

    def step(S, xs):
        qc, kc, vc, gc = xs
        b = jnp.cumsum(gc, axis=2)
        o_inter = jnp.einsum('bhtd,bhde->bhte', qc * jnp.exp(b), S)
        diff = b[:, :, :, None, :] - b[:, :, None, :, :]
        decay = jnp.where(causal, jnp.exp(jnp.where(causal, diff, 0.0)), 0.0)
        a = jnp.einsum('bhtd,bhsd,bhtsd->bhts', qc, kc, decay)
        o = o_inter + jnp.einsum('bhts,bhse->bhte', a, vc)
        b_last = b[:, :, -1:, :]
        S = jnp.exp(b_last[:, :, 0, :])[..., None] * S + jnp.einsum(
            'bhsd,bhse->bhde', kc * jnp.exp(b_last - b), vc)
        return S, o

    S, o = lax.scan(step, s0, (qs, ks, vs, gs))
    o = o.transpose(1, 0, 3, 2, 4).reshape(B, n * chunk, H, DV)[:, :L]
    return o, S


def hgrn_mixer(h, s0, w_in, lb, gnorm, w_out):
    B, L, _ = h.shape
    proj = h @ w_in
    q, f, i, g = jnp.split(proj, [F_DIM, 2 * F_DIM, 2 * F_DIM + V_DIM], axis=-1)
    q = jax.nn.silu(q.astype(jnp.float32)) * (HEAD_K ** -0.5)
    f = f.astype(jnp.float32)
    lbf = lb.astype(jnp.float32)
    log_f = jnp.logaddexp(jnp.log(lbf), jnp.log1p(-lbf) + jax.nn.log_sigmoid(f))
    k = (1.0 - lbf) * jax.nn.sigmoid(-f)
    hq = q.reshape(B, L, N_HEADS, HEAD_K)
    hk = k.reshape(B, L, N_HEADS, HEAD_K)
    hf = log_f.reshape(B, L, N_HEADS, HEAD_K)
    hv = i.astype(jnp.float32).reshape(B, L, N_HEADS, HEAD_V)
    o, s_new = gla_chunked(hq, hk, hv, hf, s0.astype(jnp.float32), min(GLA_CHUNK, L))
    o = rmsnorm(o, gnorm) * jax.nn.silu(g.astype(jnp.float32).reshape(B, L, N_HEADS, HEAD_V))
    y = o.reshape(B, L, V_DIM).astype(h.dtype) @ w_out
    return y, s_new.astype(s0.dtype)


def conv_ffn(h, ctx, w_up, conv_w, conv_b, w_down):
    L = h.shape[1]
    u = h @ w_up
    full = jnp.concatenate([ctx.astype(u.dtype), u], axis=1)
    c = conv_b + sum(full[:, j:j + L] * conv_w[j] for j in range(CONV_W))
    gate, val = jnp.split(c, 2, axis=-1)
    y = (jax.nn.gelu(gate, approximate=True) * val) @ w_down
    return y, full[:, -(CONV_W - 1):]


def trunk(x, pos0, pool_ctx, hgrn_s, ffn_ctx, norm_mix_pre, norm_mix_post, norm_ffn_pre,
          norm_ffn_post, pool_w, pool_scale, hgrn_w_in, hgrn_lb_logits, hgrn_gnorm, hgrn_w_out,
          ffn_w_up, ffn_conv_w, ffn_conv_b, ffn_w_down):
    lb_all = jnp.cumsum(jax.nn.softmax(hgrn_lb_logits.astype(jnp.float32), axis=0), axis=0)
    lb_all = lb_all - lb_all[0:1]
    new_pool, new_hgrn, new_ffn = [], [], []
    for li in range(DEPTH):
        j = li // N_MIXERS
        h = rmsnorm(x, norm_mix_pre[li])
        if li % N_MIXERS == 0:
            m, st = pool_mixer(h, pool_ctx[j], pos0, pool_w[j], pool_scale[j])
            new_pool.append(st)
        else:
            m, st = hgrn_mixer(h, hgrn_s[j], hgrn_w_in[j], lb_all[li], hgrn_gnorm[j], hgrn_w_out[j])
            new_hgrn.append(st)
        x = x + rmsnorm(m, norm_mix_post[li])
        h = rmsnorm(x, norm_ffn_pre[li])
        m, st = conv_ffn(h, ffn_ctx[li], ffn_w_up[li], ffn_conv_w[li], ffn_conv_b[li], ffn_w_down[li])
        new_ffn.append(st)
        x = x + rmsnorm(m, norm_ffn_post[li])
    return x, jnp.stack(new_pool), jnp.stack(new_hgrn), jnp.stack(new_ffn)


def setup_inputs(seed: int = 0) -> dict:
    key = jax.random.key(seed)
    ks = jax.random.split(key, 24)
    f32 = jnp.float32
    nrm = lambda k, shape, s: jax.random.normal(k, shape, f32) * s
    return {
        "x_prompt": nrm(ks[0], (BATCH, SEQ, D_MODEL), 1.0),
        "x_sample": nrm(ks[1], (DEC_BATCH, DEC_SEQ, D_MODEL), 1.0),
        "state_pool": nrm(ks[2], (N_POOL, DEC_BATCH, POOL_CTX, D_MODEL), 1.0),
        "state_hgrn": nrm(ks[3], (N_HGRN, DEC_BATCH, N_HEADS, HEAD_K, HEAD_V), 0.5),
        "state_ffn_conv": nrm(ks[4], (DEPTH, DEC_BATCH, CONV_W - 1, 2 * D_FF), 1.0),
        "norm_mix_pre": 1.0 + nrm(ks[5], (DEPTH, D_MODEL), 0.05),
        "norm_mix_post": 1.0 + nrm(ks[6], (DEPTH, D_MODEL), 0.05),
        "norm_ffn_pre": 1.0 + nrm(ks[7], (DEPTH, D_MODEL), 0.05),
        "norm_ffn_post": 1.0 + nrm(ks[8], (DEPTH, D_MODEL), 0.05),
        "pool_w": nrm(ks[9], (N_POOL, POOL_GROUPS, POOL_GROUP_DIM, POOL_GROUP_DIM), POOL_GROUP_DIM ** -0.5),
        "pool_scale": 1.0 + nrm(ks[10], (N_POOL, D_MODEL), 0.1),
        "hgrn_w_in": nrm(ks[11], (N_HGRN, D_MODEL, 2 * F_DIM + 2 * V_DIM), D_MODEL ** -0.5),
        "hgrn_lb_logits": nrm(ks[12], (DEPTH, F_DIM), 0.5),
        "hgrn_gnorm": 1.0 + nrm(ks[13], (N_HGRN, HEAD_V), 0.05),
        "hgrn_w_out": nrm(ks[14], (N_HGRN, V_DIM, D_MODEL), V_DIM ** -0.5),
        "ffn_w_up": nrm(ks[15], (DEPTH, D_MODEL, 2 * D_FF), D_MODEL ** -0.5),
        "ffn_conv_w": nrm(ks[16], (DEPTH, CONV_W, 2 * D_FF), 0.5),
        "ffn_conv_b": nrm(ks[17], (DEPTH, 2 * D_FF), 0.02),
        "ffn_w_down": nrm(ks[18], (DEPTH, D_FF, D_MODEL), D_FF ** -0.5),
    }


def reference(x_prompt, x_sample, state_pool, state_hgrn, state_ffn_conv, norm_mix_pre,
              norm_mix_post, norm_ffn_pre, norm_ffn_post, pool_w, pool_scale, hgrn_w_in,
              hgrn_lb_logits, hgrn_gnorm, hgrn_w_out, ffn_w_up, ffn_conv_w, ffn_conv_b, ffn_w_down):
    weights = (norm_mix_pre, norm_mix_post, norm_ffn_pre, norm_ffn_post, pool_w, pool_scale,
               hgrn_w_in, hgrn_lb_logits, hgrn_gnorm, hgrn_w_out, ffn_w_up, ffn_conv_w,
               ffn_conv_b, ffn_w_down)
    dt = x_prompt.dtype
    pool0 = jnp.zeros((N_POOL, BATCH, POOL_CTX, D_MODEL), dt)
    hgrn0 = jnp.zeros((N_HGRN, BATCH, N_HEADS, HEAD_K, HEAD_V), dt)
    ffn0 = jnp.zeros((DEPTH, BATCH, CONV_W - 1, 2 * D_FF), dt)
    y_prompt, pool_p, hgrn_p, ffn_p = trunk(x_prompt, 0, pool0, hgrn0, ffn0, *weights)
    y_sample, pool_s, hgrn_s, ffn_s = trunk(x_sample, PAST_LEN, state_pool, state_hgrn,
                                            state_ffn_conv, *weights)
    return (y_prompt, y_sample, pool_p, pool_s, hgrn_p, hgrn_s, ffn_p, ffn_s)
```

```python
import os
import numpy as np
from contextlib import ExitStack
import concourse.bass as bass
import concourse.mybir as mybir
from concourse.bass_utils import run_bass_kernel_spmd

F32, BF16 = mybir.dt.float32, mybir.dt.bfloat16
AF = mybir.ActivationFunctionType
ALU = mybir.AluOpType

NCORES = 8
D = 1024
SEQ = 2048
NSS = 16
DFF = 2816
NJ = 22
TS = 704
NST = 3
EPS = 1e-6
STAGES = os.environ.get("MK_STAGES", "pool,ffn0,hgrn,ffn1").split(",")
VAL_POOL = os.environ.get("VAL_POOL", "0") == "1"
PROD_ENG = os.environ.get("PROD_ENG", "dve")
HG_LEAD = int(os.environ.get("HG_LEAD", "12"))
HG_SKIP = os.environ.get("HG_SKIP", "").split(",")


def bcast(ap, axis, n):
    l = [list(x) for x in ap.ap]
    assert l[axis][1] == 1, (l, axis)
    l[axis] = [0, n]
    return bass.AP(tensor=ap.tensor, offset=ap.offset, ap=l)


class _Rec:
    def __getattr__(self, name):
        def f(*a, **kw):
            self.call = (name, a, kw)
            return self
        return f


class Op:
    __slots__ = ("eng", "fn", "deps", "is_dma", "sem", "semval", "sig", "pos", "sigval")


class Prog:
    ENGS = ("pe", "act", "dve", "pool", "sp")

    def __init__(self, nc, es):
        self.nc = nc
        self.es = es
        self.ops = {e: [] for e in self.ENGS}
        self.res = {}
        self.dma_sems = {}
        self.pending_dmas = []
        self.bar = {}
        self.free_sems = {}
        self.sem_eng = {}
        self.all_sems = []
        self.esem = {e: es.enter_context(nc.semaphore("sem_" + e)) for e in ("pe", "act", "dve", "pool")}

    def add(self, eng, fn, reads=(), writes=(), sig=True, dma_key=None):
        op = Op()
        rec = _Rec()
        fn(rec)
        name_, a_, kw_ = rec.call
        fn = lambda e, name_=name_, a_=a_, kw_=kw_: getattr(e, name_)(*a_, **kw_)
        op.eng, op.fn, op.sig, op.is_dma = eng, fn, sig, dma_key is not None
        op.deps = []
        op.sem = None
        op.semval = 0
        op.sigval = 0
        if self.bar.get(eng):
            op.deps += [(d, "bar") for d in self.bar.pop(eng)]
        for k in reads:
            st = self.res.setdefault(k, [None, []])
            if st[0] is not None:
                op.deps.append((st[0], "raw"))
            st[1].append(op)
        for k in writes:
            st = self.res.setdefault(k, [None, []])
            if st[0] is not None and st[0] is not op:
                op.deps.append((st[0], "waw"))
            for r in st[1]:
                if r is not op:
                    op.deps.append((r, "war"))
            st[0] = op
            st[1] = []
        if op.is_dma:
            if dma_key not in self.dma_sems:
                fl = self.free_sems.setdefault(eng, [])
                if fl:
                    self.dma_sems[dma_key] = fl.pop()
                else:
                    ent = [self.es.enter_context(self.nc.semaphore("dq%d" % len(self.all_sems))), 0]
                    self.all_sems.append(ent)
                    self.dma_sems[dma_key] = ent
            ent = self.dma_sems[dma_key]
            self.sem_eng[id(ent)] = eng
            ent[1] += 16
            op.sem, op.semval = ent[0], ent[1]
            self.pending_dmas.append(op)
        op.pos = len(self.ops[eng])
        self.ops[eng].append(op)
        return op

    def barrier(self):
        deps = list(self.pending_dmas)
        self.pending_dmas = []
        for e in ("pe", "act", "dve", "pool"):
            for op in reversed(self.ops[e]):
                if not op.is_dma:
                    deps.append(op)
                    break
        self.bar = {e: list(deps) for e in self.ENGS}
        for ent in self.dma_sems.values():
            self.free_sems.setdefault(self.sem_eng[id(ent)], []).append(ent)
        self.dma_sems = {}

    def dma(self, eng, out, in_, reads, writes, key):
        return self.add(eng, lambda e: e.dma_start(out=out, in_=in_), reads, writes, dma_key=key)

    def op(self, eng, meth, reads, writes, sig=True, **kw):
        return self.add(eng, lambda e: getattr(e, meth)(**kw), reads, writes, sig=sig)

    def emit(self):
        nc = self.nc
        nextsig = {}
        for e in ("pe", "act", "dve", "pool"):
            c = 0
            for op in self.ops[e]:
                if not op.is_dma and op.sig:
                    c += 1
                    op.sigval = c
            nxt = None
            arr = [0] * len(self.ops[e])
            for i in range(len(self.ops[e]) - 1, -1, -1):
                op = self.ops[e][i]
                if not op.is_dma and op.sig:
                    nxt = op.sigval
                arr[i] = nxt
            nextsig[e] = arr

        def run(e, engobj):
            seen = {}
            for op in self.ops[e]:
                need = {}
                for d, kind in op.deps:
                    if d.is_dma:
                        sem, val = d.sem, d.semval
                    else:
                        if d.eng == e and not op.is_dma:
                            if e == "pe":
                                continue
                        sem, val = self.esem[d.eng], nextsig[d.eng][d.pos]
                        assert val is not None
                    if val > need.get(sem.num, (None, 0))[1]:
                        need[sem.num] = (sem, val)
                for num, (sem, val) in need.items():
                    if seen.get(num, 0) >= val:
                        continue
                    engobj.wait_ge(sem, val)
                    seen[num] = val
                inst = op.fn(engobj)
                if op.is_dma:
                    inst.then_inc(op.sem, 16)
                elif op.sig:
                    inst.then_inc(self.esem[e], 1)
            if e == "sp":
                for sem, cnt in self.all_sems:
                    if seen.get(sem.num, 0) < cnt:
                        engobj.wait_ge(sem, cnt)

        with nc.Block() as block:
            @block.tensor
            def _(t):
                run("pe", t)

            @block.scalar
            def _(a):
                run("act", a)

            @block.vector
            def _(v):
                run("dve", v)

            @block.gpsimd
            def _(g):
                run("pool", g)

            @block.sync
            def _(s):
                run("sp", s)


def build_program():
    nc = bass.Bass("TRN2", target_bir_lowering=False)
    es = ExitStack()
    dt_in = lambda name, shape: nc.dram_tensor(name, list(shape), F32, kind="ExternalInput").ap()
    dt_out = lambda name, shape: nc.dram_tensor(name, list(shape), F32, kind="ExternalOutput").ap()
    xp = dt_in("xp", (SEQ, D))
    xs = dt_in("xs", (64, D))
    st_pool = dt_in("st_pool", (NSS, 15, D))
    st_hgrn = dt_in("st_hgrn", (NSS, 8, 128, 128))
    st_ffn = dt_in("st_ffn", (2, NSS * 2, 2 * DFF))
    n_mpre = dt_in("n_mpre", (2, D))
    n_mpost = dt_in("n_mpost", (2, D))
    n_fpre = dt_in("n_fpre", (2, D))
    n_fpost = dt_in("n_fpost", (2, D))
    pool_w = dt_in("pool_w", (4, 256, 256))
    pool_scale = dt_in("pool_scale", (1, D))
    w_in = dt_in("w_in", (8, 128, 4096))
    lb_logits = dt_in("lb_logits", (2, D))
    gnorm = dt_in("gnorm", (1, 128))
    w_out = dt_in("w_out", (D, D))
    w_up = dt_in("w_up", (2, NJ, 128, 2048))
    conv_w = dt_in("conv_w", (6, 2 * DFF))
    conv_b = dt_in("conv_b", (2, 2 * DFF))
    w_down = dt_in("w_down", (2, DFF, D))
    c_ident = dt_in("c_ident", (128, 128))
    c_band = dt_in("c_band", (128, 16 * 128))
    c_bs = dt_in("c_bs", (128, 4 * 24))
    c_tri = dt_in("c_tri", (64, 64))
    c_blk = dt_in("c_blk", (64, 64))
    c_seqm = dt_in("c_seqm", (64, 16))
    c_rst = dt_in("c_rst", (128, 2 * TS))

    yp = dt_out("yp", (SEQ, D))
    ys = dt_out("ys", (64, D))
    o_pool_p = dt_out("o_pool_p", (15, D))
    o_pool_s = dt_out("o_pool_s", (NSS, 15, D))
    o_hgrn_p = dt_out("o_hgrn_p", (8, 128, 128))
    o_hgrn_s = dt_out("o_hgrn_s", (NSS, 8, 128, 128))
    o_ffn_p = dt_out("o_ffn_p", (2, 2, 2 * DFF))
    o_ffn_s = dt_out("o_ffn_s", (2, NSS * 2, 2 * DFF))

    pg = Prog(nc, es)
    sb = lambda name, shape, dt=F32: es.enter_context(nc.sbuf_tensor(name, list(shape), dt))
    X = sb("X", (128, 18, D))
    ident_f = sb("ident_f", (128, 128))
    ident_b = sb("ident_b", (128, 128), BF16)
    gb = [sb("gb%d" % i, (128, D)) for i in range(2)]
    T2 = sb("T2", (128, D))
    T2b = T2.bitcast(BF16)
    hb2 = [T2b[:, 0:D], T2b[:, D:2 * D]]
    T2K = ["T2", ("T2h", 0), ("T2h", 1)]
    hb_rr = [0]
    junk = sb("junk", (128, D), BF16)
    stat = sb("stat", (128, 64))
    neghalf = sb("neghalf", (128, 8))
    pp = [es.enter_context(nc.psum_tensor("pp%d" % i, [128, 1024], F32)) for i in range(4)]
    ppb = [p.bitcast(BF16) for p in pp]

    blocks = lambda s: [(s * 6 + b, 128 if b < 5 else 64, b) for b in range(6)]

    pg.dma("sp", ident_f[:], c_ident, [], ["ident_f"], "c0")
    pg.dma("pool", ident_b[:], c_ident, [], ["ident_b"], "c1")
    pg.add("dve", lambda e: e.memset(neghalf[:], -0.5), [], ["neghalf"])
    pg.add("dve", lambda e: e.memset(stat[:], 1.0), [], [("stat", c) for c in (0, 6, 12, 8, 14, 20, 60)] + [("stat", c) for c in range(32, 48)])

    def load_x():
        for s in range(NST):
            for b in range(5):
                r0 = s * TS + b * 128
                if s == 0 and b == 0 and "pool" in STAGES:
                    continue
                pg.dma("sp", X[:, s * 6 + b, :], xp[r0:r0 + 128, :], [], [("X", s * 6 + b)], ("xl", s, b))
            if s < 2:
                pg.dma("sp", X[0:64, s * 6 + 5, :], xp[s * TS + 640:s * TS + 704, :], [], [("X", s * 6 + 5)], ("xl5", s))
            else:
                pg.dma("sp", X[0:64, 17, :], xs, [], [("X", 17)], ("xl5", s))


    def load_gb(slot, src, row):
        a = bass.AP(tensor=src.tensor, offset=row * D, ap=[[0, 128], [1, D]])
        pg.dma("sp", gb[slot][:], a, [], [("gb", slot)], ("gb", slot))

    uid_c = [0]

    def uid():
        uid_c[0] += 1
        return uid_c[0]

    pp_rr = [0]

    def next_pp():
        i = pp_rr[0] % 4
        pp_rr[0] += 1
        return i

    st_rr = [0]

    def stats_square(s, col0, bi):
        (B, np_, b) = blocks(s)[bi]
        pg.add("act", lambda e: e.activation(
            out=junk[:np_, :], in_=X[:np_, B, :], func=AF.Square, accum_out=stat[:np_, col0 + b:col0 + b + 1]),
            [("X", B)], ["junk", ("stat", col0)])

    def stats_final(col0, n=6):
        pg.add("dve", lambda e: e.tensor_scalar(out=stat[:, col0:col0 + n], in0=stat[:, col0:col0 + n],
                                                scalar1=1.0 / D, scalar2=EPS, op0=ALU.mult, op1=ALU.add),
               [("stat", col0)], [("stat", col0)])
        pg.add("pool", lambda e: e.tensor_tensor(out=stat[:, col0:col0 + n], in0=stat[:, col0:col0 + n],
                                                 in1=neghalf[:, 0:n], op=ALU.pow),
               [("stat", col0), "neghalf"], [("stat", col0)])

    def prenorm_stats(s, col0):
        for (B, np_, b) in blocks(s):
            pg.add("act", lambda e, B=B, np_=np_, b=b: e.activation(
                out=junk[:np_, :], in_=X[:np_, B, :], func=AF.Square, accum_out=stat[:np_, col0 + b:col0 + b + 1]),
                [("X", B)], ["junk", ("stat", col0)])
        pg.add("dve", lambda e: e.tensor_scalar(out=stat[:, col0:col0 + 6], in0=stat[:, col0:col0 + 6],
                                                scalar1=1.0 / D, scalar2=EPS, op0=ALU.mult, op1=ALU.add),
               [("stat", col0)], [("stat", col0)])
        pg.add("pool", lambda e: e.tensor_tensor(out=stat[:, col0:col0 + 6], in0=stat[:, col0:col0 + 6],
                                                 in1=neghalf[:, 0:6], op=ALU.pow),
               [("stat", col0), "neghalf"], [("stat", col0)])

    def postnorm_residual(ppi, B, np_, gslot, scale_slot=None, T1=None):
        c = 32 + (st_rr[0] % 16)
        st_rr[0] += 1
        src = pp[ppi][:np_, :]
        srck = ("pp", ppi)
        if scale_slot is not None:
            pg.add("dve", lambda e: e.tensor_tensor(out=T1[:np_, :], in0=src, in1=gb[scale_slot][:np_, :], op=ALU.mult),
                   [srck, ("gb", scale_slot)], ["T1"])
            src = T1[:np_, :]
            srck = "T1"
        pg.add("act", lambda e: e.activation(out=junk[:np_, :], in_=src, func=AF.Square,
                                             accum_out=stat[:np_, c:c + 1]), [srck], ["junk", ("stat", c)])
        pg.add("dve", lambda e: e.tensor_scalar(out=stat[:np_, c:c + 1], in0=stat[:np_, c:c + 1],
                                                scalar1=1.0 / D, scalar2=EPS, op0=ALU.mult, op1=ALU.add),
               [("stat", c)], [("stat", c)])
        pg.add("pool", lambda e: e.tensor_tensor(out=stat[:np_, c:c + 1], in0=stat[:np_, c:c + 1],
                                                 in1=neghalf[:np_, 0:1], op=ALU.pow),
               [("stat", c), "neghalf"], [("stat", c)])
        pg.add("dve", lambda e: e.scalar_tensor_tensor(out=T2[:np_, :], in0=src, scalar=stat[:np_, c:c + 1],
                                                       in1=gb[gslot][:np_, :], op0=ALU.mult, op1=ALU.mult),
               [srck, ("stat", c), ("gb", gslot)], T2K)
        pg.add("pool", lambda e: e.tensor_tensor(out=X[:np_, B, :], in0=X[:np_, B, :], in1=T2[:np_, :], op=ALU.add),
               [("X", B)] + T2K, [("X", B)])

    def pool_stage():
        with ExitStack() as ps:
            sbp = lambda name, shape, dt=F32: ps.enter_context(nc.sbuf_tensor("%s_%d" % (name, uid()), list(shape), dt))
            PT = sbp("PT", (128, 8, TS), BF16)
            H32 = [sbp("H32_%d" % i, (128, D)) for i in range(2)]
            Hhi = [sbp("Hhi%d" % i, (128, D), BF16) for i in range(2)]
            Hlo = [sbp("Hlo%d" % i, (128, D), BF16) for i in range(2)]
            FULLS = [sbp("FULL%d" % i, (128, D)) for i in range(3)]
            H32s = sbp("H32s", (64, D))
            T1 = sbp("T1", (128, D))
            wpool = sbp("wpool", (128, 4, 2, 256), BF16)
            BND = sbp("BND", (128, 16, 128), BF16)
            BS = sbp("BS", (128, 4, 24), BF16)
            gbs = sbp("gbs", (128, D))
            pg.dma("sp", X[:, 0, :], xp[0:128, :], [], [("X", 0)], ("xl", 0, 0))
            load_gb(0, n_mpre, 0)
            load_gb(1, n_mpost, 0)
            pg.dma("pool", wpool[:], pool_w.rearrange("g (dc dp) e -> dp g dc e", dp=128), [], ["wpool"], "wpool")
            pg.dma("pool", BND[:], c_band.rearrange("p (m t) -> p m t", t=128), [], ["BND"], "bnd")
            pg.dma("pool", BS[:], c_bs.rearrange("p (m t) -> p m t", t=24), [], ["BS"], "bs")
            pg.dma("sp", gbs[:], bass.AP(tensor=pool_scale.tensor, offset=0, ap=[[0, 128], [1, D]]), [], ["gbs"], "gbs")
            load_x()
            pg.dma("sp", o_pool_s[:, 0:11, :], st_pool[:, 4:15, :], [], [], "poolctx")
            hrr = [0]

            class Back:
                def __init__(self, B, np_, tok0):
                    self.B, self.np_, self.tok0 = B, np_, tok0
                    self.c = 32 + (st_rr[0] % 16)
                    st_rr[0] += 1

                def mm(self):
                    self.ppi = next_pp()
                    ppi, np_, tok0 = self.ppi, self.np_, self.tok0
                    for g in range(4):
                        for dc in range(2):
                            pg.add("pe", lambda e: e.matmul(
                                out=pp[ppi][:np_, g * 256:(g + 1) * 256], lhsT=PT[:, 2 * g + dc, tok0:tok0 + np_],
                                rhs=wpool[:, g, dc, :], start=(dc == 0), stop=(dc == 1)),
                                ["PT", "wpool"], [("pp", ppi)], sig=(g == 3 and dc == 1))

                def t1(self):
                    ppi, np_ = self.ppi, self.np_
                    pg.add("dve", lambda e: e.tensor_tensor(out=T1[:np_, :], in0=pp[ppi][:np_, :], in1=gbs[:np_, :], op=ALU.mult),
                           [("pp", ppi), "gbs"], ["T1"])

                def sq(self):
                    np_, c = self.np_, self.c
                    pg.add("act", lambda e: e.activation(out=junk[:np_, :], in_=T1[:np_, :], func=AF.Square,
                                                         accum_out=stat[:np_, c:c + 1]), ["T1"], ["junk", ("stat", c)])

                def st(self):
                    np_, c = self.np_, self.c
                    pg.add("dve", lambda e: e.tensor_scalar(out=stat[:np_, c:c + 1], in0=stat[:np_, c:c + 1],
                                                            scalar1=1.0 / D, scalar2=EPS, op0=ALU.mult, op1=ALU.add),
                           [("stat", c)], [("stat", c)])
                    pg.add("pool", lambda e: e.tensor_tensor(out=stat[:np_, c:c + 1], in0=stat[:np_, c:c + 1],
                                                             in1=neghalf[:np_, 0:1], op=ALU.pow),
                           [("stat", c), "neghalf"], [("stat", c)])

                def t2(self):
                    np_, c, B = self.np_, self.c, self.B
                    pg.add("dve", lambda e: e.scalar_tensor_tensor(out=T2[:np_, :], in0=T1[:np_, :], scalar=stat[:np_, c:c + 1],
                                                                   in1=gb[1][:np_, :], op0=ALU.mult, op1=ALU.mult),
                           ["T1", ("stat", c), ("gb", 1)], T2K)
                    pg.add("pool", lambda e: e.tensor_tensor(out=X[:np_, B, :], in0=X[:np_, B, :], in1=T2[:np_, :], op=ALU.add),
                           [("X", B)] + T2K, [("X", B)])

                def all(self):
                    self.mm(); self.t1(); self.sq(); self.st(); self.t2()

            class Src:
                def __init__(self, t, key):
                    self.t, self.name_key = t, key

                def __getitem__(self, idx):
                    return self.t[idx]

            def hi(src, hs, rows):
                pg.add("act", lambda e: e.activation(out=Hhi[hs][:rows, :], in_=src[:rows, :], func=AF.Copy), src.name_key, [("Hhi", hs)])

            def lo(src, hs, rows):
                pg.add("dve", lambda e: e.tensor_tensor(out=Hlo[hs][:rows, :], in0=src[:rows, :], in1=Hhi[hs][:rows, :], op=ALU.subtract),
                       src.name_key + [("Hhi", hs)], [("Hlo", hs)])

            prev = None
            back = None
            for s in range(NST):
                if s > 0 and back is not None:
                    back.all()
                    back = None
                if s == 0:
                    prenorm_stats(0, 0)
                    for kb in range(3):
                        pg.add("dve", lambda e: e.memset(FULLS[kb][:], 0.0), [], [("FULL", kb, i) for i in range(12)])
                if s == 1:
                    pg.add("act", lambda e: e.activation(out=junk[:64, :], in_=X[:64, 17, :], func=AF.Square,
                                                         accum_out=stat[:64, 60:61]), [("X", 17)], ["junk", ("stat", 60)])
                    stats_final(60, 1)
                    pg.add("dve", lambda e: e.scalar_tensor_tensor(
                        out=H32s[:64, :], in0=X[:64, 17, :], scalar=stat[:64, 60:61], in1=gb[0][:64, :],
                        op0=ALU.mult, op1=ALU.mult), [("X", 17), ("stat", 60), ("gb", 0)], ["H32s"])
                    for q in range(NSS):
                        pg.dma("sp", o_pool_s[q, 11:15, :], H32s[4 * q:4 * q + 4, :], ["H32s"], [], ("ops", q % 4))
                    for kb, (q0, nsq) in enumerate(((0, 6), (6, 6), (12, 4))):
                        for ql in range(nsq):
                            pg.dma("sp", FULLS[kb][ql * 19:ql * 19 + 15, :], st_pool[q0 + ql], [], [("FULL", kb, 2 * ql)], ("fl", kb))
                            pg.dma("sp", FULLS[kb][ql * 19 + 15:ql * 19 + 19, :], H32s[4 * (q0 + ql):4 * (q0 + ql) + 4, :],
                                   ["H32s"], [("FULL", kb, 2 * ql + 1)], ("fl", kb))
                for (B, np_, b) in blocks(s):
                    hs = hrr[0] % 2
                    hrr[0] += 1
                    sample = (s == 2 and b == 5)
                    if s + 1 < NST:
                        stats_square(s + 1, 6 * (s + 1), b)
                        if b == 5:
                            stats_final(6 * (s + 1))
                    if sample:
                        if back is not None:
                            back.all()
                            back = None
                        for kb, (q0, nsq) in enumerate(((0, 6), (6, 6), (12, 4))):
                            fs = hrr[0] % 2
                            hrr[0] += 1
                            fkeys = [("FULL", kb, i) for i in range(12)]
                            hi(Src(FULLS[kb], fkeys), fs, 128)
                            lo(Src(FULLS[kb], fkeys), fs, 128)
                            pj = next_pp()
                            for c in range(8):
                                g = c // 2
                                for ti_, Ht in enumerate((Hhi[fs], Hlo[fs])):
                                    pg.add("pe", lambda e: e.matmul(
                                        out=pp[pj][:, c * 128:c * 128 + nsq * 4], lhsT=Ht[:, c * 128:(c + 1) * 128],
                                        rhs=BS[:, g, 0:nsq * 4], start=(ti_ == 0), stop=(ti_ == 1)),
                                        [("Hhi", fs), ("Hlo", fs), "BS"], [("pp", pj)], sig=(c == 7 and ti_ == 1))
                            pg.add("act", lambda e: e.activation(
                                out=PT[:, :, 640 + q0 * 4:640 + (q0 + nsq) * 4],
                                in_=pp[pj][:, :].rearrange("p (c t) -> p c t", c=8)[:, :, 0:nsq * 4], func=AF.Copy),
                                [("pp", pj)], ["PT"])
                        Back(B, np_, 640).all()
                        continue
                    pg.add("dve", lambda e: e.scalar_tensor_tensor(
                        out=H32[hs][:np_, :], in0=X[:np_, B, :], scalar=stat[:np_, 6 * s + b:6 * s + b + 1], in1=gb[0][:np_, :],
                        op0=ALU.mult, op1=ALU.mult), [("X", B), ("stat", 6 * s), ("gb", 0)], [("H32", hs)])
                    if s == 2 and b == 4:
                        pg.dma("sp", o_pool_p, H32[hs][113:128, :], [("H32", hs)], [], ("opp", hs))
                    if np_ == 64:
                        pg.add("dve", lambda e: e.memset(Hhi[hs][64:128, :], 0.0), [], [("Hhi", hs)])
                        pg.add("dve", lambda e: e.memset(Hlo[hs][64:128, :], 0.0), [], [("Hlo", hs)])
                    hsrc = Src(H32[hs], [("H32", hs)])
                    hi(hsrc, hs, np_)
                    if back is not None:
                        back.mm()
                        back.t1()
                        back.sq()
                    lo(hsrc, hs, np_)
                    if s == 0 and b == 0:
                        terms = [(Hhi[hs], ("Hhi", hs), 12), (Hlo[hs], ("Hlo", hs), 12)]
                    else:
                        phs, pnp = prev
                        pm = 4 if pnp == 128 else 8
                        terms = [(Hhi[hs], ("Hhi", hs), 0), (Hlo[hs], ("Hlo", hs), 0),
                                 (Hhi[phs], ("Hhi", phs), pm), (Hlo[phs], ("Hlo", phs), pm)]
                    pj = next_pp()
                    for c in range(8):
                        g = c // 2
                        for ti_, (Ht, hk, mbase) in enumerate(terms):
                            pg.add("pe", lambda e: e.matmul(
                                out=pp[pj][:, c * 128:c * 128 + np_], lhsT=Ht[:, c * 128:(c + 1) * 128],
                                rhs=BND[:, mbase + g, 0:np_], start=(ti_ == 0), stop=(ti_ == len(terms) - 1)),
                                [hk, "BND"], [("pp", pj)], sig=(c == 7 and ti_ == len(terms) - 1))
                    if back is not None:
                        back.st()
                    pg.add("act", lambda e: e.activation(
                        out=PT[:, :, b * 128:b * 128 + np_],
                        in_=pp[pj][:, :].rearrange("p (c t) -> p c t", c=8)[:, :, 0:np_], func=AF.Copy),
                        [("pp", pj)], ["PT"])
                    if back is not None:
                        back.t2()
                    back = Back(B, np_, b * 128)
                    prev = (hs, np_)

    def store_x_block(s, B, np_, b):
        if b < 5:
            r0 = s * TS + b * 128
            pg.dma("sp", yp[r0:r0 + 128, :], X[:, B, :], [("X", B)], [], ("xs", B % 4))
        elif s < 2:
            pg.dma("sp", yp[s * TS + 640:s * TS + 704, :], X[0:64, B, :], [("X", B)], [], ("xs", B % 4))
        else:
            pg.dma("sp", ys, X[0:64, 17, :], [("X", 17)], [], ("xs", B % 4))

    def ffn_stage(li, final=False):
        with ExitStack() as ps:
            sbp = lambda name, shape, dt=F32: ps.enter_context(nc.sbuf_tensor("%s_%d" % (name, uid()), list(shape), dt))
            hT = sbp("hT", (128, 8, TS + 2), BF16)
            GT = sbp("GT", (128, NJ, TS), BF16)
            NW = 3
            wup = [sbp("wup%d" % i, (128, 8, 2, 128), BF16) for i in range(NW)]
            wdn = sbp("wdn", (128, NJ, D), BF16)
            Tg = [sbp("Tg%d" % i, (128, 512)) for i in range(3)]
            Tv = [sbp("Tv%d" % i, (128, 512)) for i in range(3)]
            Tx = [sbp("Tx%d" % i, (128, 512)) for i in range(2 if (VAL_POOL or PROD_ENG == "pool32") else 0)]
            hbf = [junk]
            cwT = sbp("cwT", (128, 44, 4))
            SU = sbp("SU", (128, 44, 34))
            FU = [sbp("FU%d" % i, (128, 16, 6)) for i in range(2)]
            load_gb(0, n_fpre, li)
            load_gb(1, n_fpost, li)
            stgf = wdn.bitcast(F32).rearrange("p j n -> p (j n)")
            wk = [("wdn", j_) for j_ in range(NJ)]
            pg.dma("sp", stgf[0:3, 0:2 * DFF], conv_w[li * 3:li * 3 + 3, :], [], wk[:11], "stga")
            pg.dma("sp", stgf[3:4, 0:2 * DFF], conv_b[li:li + 1, :], [], wk[:11], "stgb")
            pg.dma("sp", stgf[0:32, 2 * DFF:4 * DFF], st_ffn[li, :, :], [], wk[11:], "stgc")
            prenorm_stats(0, 8)
            trr = [0]
            frr = [0]
            NT = len(Tg)
            pairs = [(s, j) for s in range(NST) for j in range(NJ)]
            def issue_w(k):
                if k >= len(pairs):
                    return
                s_, j_ = pairs[k]
                ws_ = k % NW
                pg.dma("pool", wup[ws_][:].rearrange("p a b c -> p (a b c)"), w_up[li, j_], [], [("wup", ws_)], ("wup", ws_, 0))

            def prenorm(s):
                ncols_ = TS + 2
                if s == 0:
                    pg.add("dve", lambda e: e.memset(hT[:, :, 0:2], 0.0), [], ["hT"])
                else:
                    pg.add("dve", lambda e: e.tensor_copy(out=hT[:, :, 0:2], in_=hT[:, :, TS:TS + 2]), ["hT"], ["hT"])
                for (B, np_, b) in blocks(s):
                    hi_ = hb_rr[0] % 2
                    hb_rr[0] += 1
                    pg.add("dve", lambda e: e.scalar_tensor_tensor(
                        out=hb2[hi_][:np_, :], in0=X[:np_, B, :], scalar=stat[:np_, 8 + 6 * s + b:9 + 6 * s + b], in1=gb[0][:np_, :],
                        op0=ALU.mult, op1=ALU.mult), [("X", B), ("stat", 8 + 6 * s), ("gb", 0)], [("T2h", hi_)])
                    pj = next_pp()
                    for kc in range(8):
                        pg.add("pe", lambda e: e.transpose(
                            out=ppb[pj][:, kc * 128:kc * 128 + np_], in_=hb2[hi_][:np_, kc * 128:(kc + 1) * 128],
                            identity=ident_b[:np_, :np_]), [("T2h", hi_), "ident_b"], [("pp", pj)], sig=(kc == 7))
                    pg.add("act", lambda e: e.activation(
                        out=hT[:, :, 2 + b * 128:2 + b * 128 + np_],
                        in_=ppb[pj][:, 0:1024].rearrange("p (c t) -> p c t", c=8)[:, :, 0:np_], func=AF.Copy),
                        [("pp", pj)], ["hT"])

            def conv_tile(s, j, ws, c0, c1, is_last_prompt):
                n = c1 - c0
                m = n - 2
                pj = next_pp()
                for half in range(2):
                    for kc in range(8):
                        pg.add("pe", lambda e: e.matmul(
                            out=pp[pj][:, half * 512:half * 512 + n], lhsT=wup[ws][:, kc, half, :],
                            rhs=hT[:, kc, c0:c1], start=(kc == 0), stop=(kc == 7)),
                            [("wup", ws), "hT"], [("pp", pj)], sig=(kc == 7 and half == 1))
                ti = trr[0] % NT
                trr[0] += 1
                hv = [(0, Tg[ti], ("Tg", ti), j), (1, Tv[ti], ("Tv", ti), NJ + j)]
                U = [pp[pj][:, half * 512:half * 512 + n] for half in range(2)]
                pk = ("pp", pj)
                for half, Tt, key, ch in hv:
                    pg.add("act", lambda e: e.activation(
                        out=Tt[:, 0:m], in_=U[half][:, 0:m], func=AF.Identity, scale=cwT[:, ch, 0:1], bias=cwT[:, ch, 3:4]),
                        [pk, "cwT"], [key])
                if VAL_POOL:
                    chv = NJ + j
                    pg.add("act", lambda e: e.activation(out=Tx[ti % 2][:, 0:m], in_=U[1][:, 1:m + 1], func=AF.Copy,
                                                         scale=cwT[:, chv, 1:2]), [pk, "cwT"], [("Tx", ti % 2)])
                    pg.add("dve", lambda e: e.scalar_tensor_tensor(
                        out=Tg[ti][:, 0:m], in0=U[0][:, 1:m + 1], scalar=cwT[:, j, 1:2], in1=Tg[ti][:, 0:m],
                        op0=ALU.mult, op1=ALU.add), [pk, "cwT", ("Tg", ti)], [("Tg", ti)])
                    pg.add("pool", lambda e: e.tensor_tensor(out=Tv[ti][:, 0:m], in0=Tv[ti][:, 0:m], in1=Tx[ti % 2][:, 0:m], op=ALU.add),
                           [("Tv", ti), ("Tx", ti % 2)], [("Tv", ti)])
                    for half, Tt, key, ch in hv:
                        pg.add("dve", lambda e: e.scalar_tensor_tensor(
                            out=Tt[:, 0:m], in0=U[half][:, 2:m + 2], scalar=cwT[:, ch, 2:3], in1=Tt[:, 0:m],
                            op0=ALU.mult, op1=ALU.add), [pk, "cwT", key], [key])
                else:
                    for tap in (1, 2):
                        for half, Tt, key, ch in hv:
                            pg.add("dve", lambda e: e.scalar_tensor_tensor(
                                out=Tt[:, 0:m], in0=U[half][:, tap:m + tap], scalar=cwT[:, ch, tap:tap + 1], in1=Tt[:, 0:m],
                                op0=ALU.mult, op1=ALU.add), [pk, "cwT", key], [key])
                if is_last_prompt:
                    for half, Tt, key, ch in hv:
                        pg.add("act", lambda e: e.activation(out=SU[:, ch, 32:34], in_=U[half][:, n - 2:n], func=AF.Copy),
                               [pk], [("SU", ch)])
                pg.add("act", lambda e: e.activation(out=Tg[ti][:, 0:m], in_=Tg[ti][:, 0:m], func=AF.Gelu_apprx_tanh),
                       [("Tg", ti)], [("Tg", ti)])
                if PROD_ENG == "pool32":
                    pg.add("pool", lambda e: e.tensor_tensor(out=Tx[ti % 2][:, 0:m], in0=Tg[ti][:, 0:m], in1=Tv[ti][:, 0:m], op=ALU.mult),
                           [("Tg", ti), ("Tv", ti)], [("Tx", ti % 2)])
                    pg.add("act", lambda e: e.activation(out=GT[:, j, c0:c0 + m], in_=Tx[ti % 2][:, 0:m], func=AF.Copy),
                           [("Tx", ti % 2)], [("GT", j)])
                else:
                    pg.add(PROD_ENG, lambda e: e.tensor_tensor(out=GT[:, j, c0:c0 + m], in0=Tg[ti][:, 0:m], in1=Tv[ti][:, 0:m], op=ALU.mult),
                           [("Tg", ti), ("Tv", ti)], [("GT", j)])

            def sample_tile(j, ws):
                pj = next_pp()
                for half in range(2):
                    for kc in range(8):
                        pg.add("pe", lambda e: e.matmul(
                            out=pp[pj][:, half * 512:half * 512 + 64], lhsT=wup[ws][:, kc, half, :],
                            rhs=hT[:, kc, 642:706], start=(kc == 0), stop=(kc == 7)),
                            [("wup", ws), "hT"], [("pp", pj)], sig=(kc == 7 and half == 1))
                ti = trr[0] % NT
                trr[0] += 1
                for half, Tt, key in ((0, Tg[ti], ("Tg", ti)), (1, Tv[ti], ("Tv", ti))):
                    ch = half * NJ + j
                    fi = frr[0] % 2
                    frr[0] += 1
                    Fu = FU[fi]
                    fk = ("FU", fi)
                    pg.add("act", lambda e: e.activation(
                        out=Fu[:, :, 0:2], in_=SU[:, ch, 0:32].rearrange("p (s r) -> p s r", r=2), func=AF.Copy),
                        [("SU", ch)], [fk])
                    pg.add("act", lambda e: e.activation(
                        out=Fu[:, :, 2:6], in_=pp[pj][:, half * 512:half * 512 + 64].rearrange("p (s t) -> p s t", t=4),
                        func=AF.Copy), [("pp", pj)], [fk])
                    To = Tt[:, 0:64].rearrange("p (s t) -> p s t", t=4)
                    pg.add("act", lambda e: e.activation(
                        out=To, in_=Fu[:, :, 0:4], func=AF.Identity, scale=cwT[:, ch, 0:1], bias=cwT[:, ch, 3:4]),
                        [fk, "cwT"], [key])
                    pg.add("dve", lambda e: e.scalar_tensor_tensor(
                        out=To, in0=Fu[:, :, 1:5], scalar=cwT[:, ch, 1:2], in1=To, op0=ALU.mult, op1=ALU.add),
                        [fk, "cwT", key], [key])
                    pg.add("dve", lambda e: e.scalar_tensor_tensor(
                        out=To, in0=Fu[:, :, 2:6], scalar=cwT[:, ch, 2:3], in1=To, op0=ALU.mult, op1=ALU.add),
                        [fk, "cwT", key], [key])
                    pg.add("act", lambda e: e.activation(
                        out=SU[:, ch, 0:32].rearrange("p (s r) -> p s r", r=2), in_=Fu[:, :, 4:6], func=AF.Copy),
                        [fk], [("SU", ch)])
                pg.add("act", lambda e: e.activation(out=Tg[ti][:, 0:64], in_=Tg[ti][:, 0:64], func=AF.Gelu_apprx_tanh),
                       [("Tg", ti)], [("Tg", ti)])
                pg.add("dve", lambda e: e.tensor_tensor(out=GT[:, j, 640:704], in0=Tg[ti][:, 0:64], in1=Tv[ti][:, 0:64], op=ALU.mult),
                       [("Tg", ti), ("Tv", ti)], [("GT", j)])

            def emit_ffn_rows():
                for g4 in range(11):
                    pj = next_pp()
                    for ci in range(4):
                        ch = g4 * 4 + ci
                        pg.add("pe", lambda e, ch=ch, ci=ci, pj=pj: e.transpose(
                            out=pp[pj][0:34, ci * 128:(ci + 1) * 128], in_=SU[:, ch, :], identity=ident_f[:, :]),
                            [("SU", ch), "ident_f"], [("pp", pj)], sig=(ci == 3))
                    oi = g4 % 2
                    pg.add("act", lambda e, oi=oi, pj=pj: e.activation(out=Tg[oi][0:34, :], in_=pp[pj][0:34, 0:512], func=AF.Copy),
                           [("pp", pj)], [("Tg", oi)])
                    pg.dma("sp", o_ffn_s[li, :, g4 * 512:(g4 + 1) * 512], Tg[oi][0:32, :], [("Tg", oi)], [], ("ofs", oi))
                    pg.dma("sp", o_ffn_p[li, :, g4 * 512:(g4 + 1) * 512], Tg[oi][32:34, :], [("Tg", oi)], [], ("ofp", oi))


            for k0 in range(NW - 1):
                issue_w(k0)
            prenorm(0)
            for q in range(11):
                pj = next_pp()
                for ci in range(4):
                    ch = q * 4 + ci
                    pg.add("pe", lambda e: e.transpose(out=pp[pj][:, ci * 4:ci * 4 + 4], in_=stgf[0:4, ch * 128:(ch + 1) * 128],
                                                       identity=ident_f[:4, :4]), [("wdn", q), "ident_f"], [("pp", pj)], sig=False)
                for ci in range(4):
                    ch = q * 4 + ci
                    pg.add("pe", lambda e: e.transpose(out=pp[pj][:, 512 + ci * 32:512 + (ci + 1) * 32],
                                                       in_=stgf[0:32, 2 * DFF + ch * 128:2 * DFF + (ch + 1) * 128],
                                                       identity=ident_f[:32, :32]), [("wdn", 11 + q), "ident_f"], [("pp", pj)], sig=(ci == 3))
                pg.add("act", lambda e: e.activation(out=cwT[:, q * 4:q * 4 + 4, :], in_=pp[pj][:, 0:16].rearrange("p (c k) -> p c k", k=4),
                                                     func=AF.Copy), [("pp", pj)], ["cwT"])
                pg.add("act", lambda e: e.activation(out=SU[:, q * 4:q * 4 + 4, 0:32],
                                                     in_=pp[pj][:, 512:640].rearrange("p (c k) -> p c k", k=32),
                                                     func=AF.Copy), [("pp", pj)], [("SU", c) for c in range(q * 4, q * 4 + 4)])

            for s in range(NST):
                ncols = TS + 2 if s < 2 else 642
                tiles = []
                c0 = 0
                while c0 + 2 < ncols:
                    c1 = min(c0 + 512, ncols)
                    tiles.append((c0, c1))
                    c0 = c1 - 2
                for j in range(NJ):
                    k = s * NJ + j
                    issue_w(k + NW - 1)
                    pg.dma("pool", wdn[:, j, :], w_down[li, j * 128:(j + 1) * 128, :], [], [("wdn", j)], ("wdn", j))
                    ws = k % NW
                    if s + 1 < NST and 2 <= j <= 12 and j % 2 == 0:
                        stats_square(s + 1, 8 + 6 * (s + 1), j // 2 - 1)
                    if s + 1 < NST and j == 14:
                        stats_final(8 + 6 * (s + 1))
                    for (c0, c1) in tiles:
                        conv_tile(s, j, ws, c0, c1, s == 2 and c1 == ncols)
                    if s == 2:
                        sample_tile(j, ws)
                if s + 1 < NST:
                    prenorm(s + 1)
                else:
                    emit_ffn_rows()
                for (B, np_, b) in blocks(s):
                    pj = next_pp()
                    for j in range(NJ):
                        for half in range(2):
                            pg.add("pe", lambda e: e.matmul(
                                out=pp[pj][:np_, half * 512:(half + 1) * 512], lhsT=GT[:, j, b * 128:b * 128 + np_],
                                rhs=wdn[:, j, half * 512:(half + 1) * 512], start=(j == 0), stop=(j == NJ - 1)),
                                [("GT", j), ("wdn", j)], [("pp", pj)], sig=(j == NJ - 1 and half == 1))
                    postnorm_residual(pj, B, np_, 1)
                    if final:
                        store_x_block(s, B, np_, b)
    held = set()

    def alloc_pp():
        for _ in range(8):
            i = next_pp()
            if i not in held:
                held.add(i)
                return i
        raise RuntimeError("all PSUM pairs held")

    def hgrn_stage():
        with ExitStack() as ps:
            sbp = lambda name, shape, dt=F32: ps.enter_context(nc.sbuf_tensor("%s_%d" % (name, uid()), list(shape), dt))
            hT = sbp("hT", (128, 8, TS), BF16)
            oT = sbp("oT", (128, 8, TS), BF16)
            NWI = 3
            win = [sbp("win%d" % i, (128, 4, 8, 128), BF16) for i in range(NWI)]
            ZDS = sbp("ZDS", (128, 3 * 1536))
            Zd = ZDS[:, 0:1536].rearrange("p (e c) -> p e c", c=12)
            D0 = ZDS[:, 1536:3072].rearrange("p (e c) -> p e c", c=12)
            Sall = ZDS[:, 3072:4608].rearrange("p (e c) -> p e c", c=12)
            wout = ZDS.bitcast(BF16)[:, 0:8192].rearrange("p (h n) -> p h n", h=8)
            ZK = ["Zd", "D0", "Sall"]
            W1 = sbp("W1", (128, TS))
            W2 = sbp("W2", (128, TS))
            W3 = sbp("W3", (128, TS))
            W4 = sbp("W4", (128, TS))
            W5 = sbp("W5", (128, TS))
            qh = [sbp("qh%d" % i, (128, TS), BF16) for i in range(2)]
            kh = [sbp("kh%d" % i, (128, TS), BF16) for i in range(2)]
            sqb = junk[:, 0:TS]
            vt = sbp("vt", (128, 11, 128), BF16)
            khT = sbp("khT", (128, 11, 128), BF16)
            AT = sbp("AT", (128, 11, 64), BF16)
            Sbf = sbp("Sbf", (128, 12, 128), BF16)
            Scar = sbp("Scar", (128, 8, 128))
            S0q = [sbp("S0q%d" % i, (128, 4, 128)) for i in range(2)]
            S0b = [sbp("S0b%d" % i, (128, 4, 128), BF16) for i in range(4)]
            Vblk = sbp("Vblk", (128, 16, 128), BF16)
            tmpS = [sbp("tmpS%d" % i, (128, 4, 128)) for i in range(2)]
            s0rr = [0]
            tri = sbp("tri", (64, 64))
            blk = sbp("blk", (64, 64))
            seqm = sbp("seqm", (128, 16))
            rst = sbp("rst", (128, 2, TS), BF16)
            hbf = [sbp("hbf%d" % i, (128, D), BF16) for i in range(1)]
            ones_b = sbp("ones_b", (128, 128), BF16)
            lbl = sbp("lbl", (128, 8, 2))
            lbs = sbp("lbs", (128, 4, 8))
            gn = sbp("gn", (128, 2))
            ebl = [sbp("ebl%d" % i, (128, 12)) for i in range(2)]
            ebs = [sbp("ebs%d" % i, (128, 16)) for i in range(2)]
            lrow = T2
            load_gb(0, n_mpre, 1)
            load_gb(1, n_mpost, 1)
            pg.dma("sp", tri[:], c_tri, [], ["tri"], "hc0")
            pg.dma("sp", blk[:], c_blk, [], ["blk"], "hc1")
            pg.add("dve", lambda e: e.memset(seqm[:], 0.0), [], ["seqm"])
            pg.dma("sp", seqm[0:64, :], c_seqm, [], ["seqm"], "hc2")
            pg.add("dve", lambda e: e.memset(vt[:], 0.0), [], ["vt"])
            pg.add("dve", lambda e: e.memset(khT[:], 0.0), [], ["khT"])
            pg.add("dve", lambda e: e.memset(AT[:], 0.0), [], ["AT"])
            pg.dma("pool", rst[:], c_rst.rearrange("p (a t) -> p a t", a=2), [], ["rst"], "hc3")
            pg.dma("sp", lrow[0:2, :], lb_logits, [], T2K, "hc4")
            pg.dma("sp", gn[:, 0:1], bass.AP(tensor=gnorm.tensor, offset=0, ap=[[1, 128], [1, 1]]), [], ["gn"], "hc5")
            pg.add("dve", lambda e: e.memset(gn[:, 1:2], EPS), [], ["gn"])
            pg.add("dve", lambda e: e.memset(ones_b[:], 1.0), [], ["ones_b"])
            pg.add("dve", lambda e: e.memset(Scar[:], 0.0), [], ["Scar"])
            for i in range(2):
                pg.add("dve", lambda e: e.memset(ebl[i][:], 0.0), [], [("ebl", i)])
            pg.add("dve", lambda e: e.memset(ZDS[:], 0.0), [], ZK + ["wout"])
            pj = next_pp()
            for c in range(8):
                pg.add("pe", lambda e: e.transpose(out=pp[pj][:, c * 2:c * 2 + 2], in_=lrow[0:2, c * 128:(c + 1) * 128],
                                                   identity=ident_f[:2, :2]), T2K + ["ident_f"], [("pp", pj)], sig=(c == 7))
            pg.add("act", lambda e: e.activation(out=lbl[:], in_=pp[pj][:, 0:16].rearrange("p (c k) -> p c k", k=2), func=AF.Copy),
                   [("pp", pj)], ["lbl"])
            pg.add("dve", lambda e: e.tensor_tensor(out=lbs[:, 2, :], in0=lbl[:, :, 1], in1=lbl[:, :, 0], op=ALU.subtract), ["lbl"], ["lbs"])
            pg.add("act", lambda e: e.activation(out=lbs[:, 3, :], in_=lbs[:, 2, :], func=AF.Sigmoid), ["lbs"], ["lbs"])
            pg.add("dve", lambda e: e.tensor_scalar(out=lbs[:, 0, :], in0=lbs[:, 3, :], scalar1=-1.0, scalar2=None, op0=ALU.add), ["lbs"], ["lbs"])
            pg.add("dve", lambda e: e.tensor_scalar(out=lbs[:, 1, :], in0=lbs[:, 3, :], scalar1=-1.0, scalar2=1.0, op0=ALU.mult, op1=ALU.add),
                   ["lbs"], ["lbs"])
            SC = float(128 ** -0.5)

            def issue_win(g):
                if g >= NST * 8:
                    return
                h_ = g % 8
                ws_ = g % NWI
                for t2 in range(2):
                    pg.dma("pool", win[ws_][:, 2 * t2:2 * t2 + 2, :, :].rearrange("p a b c -> p (a b c)"),
                           w_in[h_, :, t2 * 2048:(t2 + 1) * 2048], [], [("win", ws_)], ("win", ws_, t2))

            def proj(ws, typ):
                pj = alloc_pp()
                for (a, b_) in ((0, 512), (512, TS)):
                    for kc in range(8):
                        pg.add("pe", lambda e: e.matmul(out=pp[pj][:, a:b_], lhsT=win[ws][:, typ, kc, :], rhs=hT[:, kc, a:b_],
                                                        start=(kc == 0), stop=(kc == 7)),
                               [("win", ws), "hT"], [("pp", pj)], sig=(kc == 7 and a == 512))
                return pj

            def stage_a(s, h, g):
                p = g % 2
                ws = g % NWI
                npc = 11 if s < 2 else 10
                ri = 0 if s < 2 else 1
                pj = proj(ws, 1)
                pk = ("pp", pj)
                yield
                pg.add("act", lambda e: e.activation(out=W1[:], in_=pp[pj][:, 0:TS], func=AF.Sigmoid, scale=-1.0), [pk], ["W1"])
                held.discard(pj)
                yield
                pg.add("dve", lambda e: e.tensor_scalar(out=W3[:], in0=W1[:], scalar1=lbs[:, 0, h:h + 1], scalar2=1.0,
                                                        op0=ALU.mult, op1=ALU.add), ["W1", "lbs"], ["W3"])
                yield
                pg.add("act", lambda e: e.activation(out=W3[:], in_=W3[:], func=AF.Ln), ["W3"], ["W3"])
                yield
                pg.add("dve", lambda e: e.tensor_tensor_scan(out=W2[:], data0=rst[:, ri, :], data1=W3[:], initial=0.0,
                                                             op0=ALU.mult, op1=ALU.add), ["W3", "rst"], ["W2"])
                yield
                pg.add("act", lambda e: e.activation(out=W3[:], in_=W2[:], func=AF.Exp, scale=-1.0), ["W2"], ["W3"])
                yield
                pg.add("dve", lambda e: e.scalar_tensor_tensor(out=kh[p][:], in0=W1[:], scalar=lbs[:, 1, h:h + 1], in1=W3[:],
                                                               op0=ALU.mult, op1=ALU.mult), ["W1", "W3", "lbs"], [("kh", p)])
                yield
                pg.add("act", lambda e: e.activation(out=W1[:], in_=W2[:], func=AF.Exp), ["W2"], ["W1"])
                yield
                pg.add("dve", lambda e: e.tensor_copy(out=ebl[p][:, 1:1 + npc],
                                                      in_=W1[:, 0:npc * 64].rearrange("p (c t) -> p c t", t=64)[:, :, 63]),
                       ["W1"], [("ebl", p)])
                if s == 2:
                    pg.add("dve", lambda e: e.tensor_copy(out=ebs[p][:],
                                                          in_=W1[:, 640:704].rearrange("p (c t) -> p c t", t=4)[:, :, 3]),
                           ["W1"], [("ebs", p)])
                yield
                pj = proj(ws, 0)
                pk = ("pp", pj)
                yield
                pg.add("act", lambda e: e.activation(out=W2[:], in_=pp[pj][:, 0:TS], func=AF.Silu), [pk], ["W2"])
                held.discard(pj)
                yield
                pg.add("dve", lambda e: e.scalar_tensor_tensor(out=qh[p][:], in0=W2[:], scalar=SC, in1=W1[:],
                                                               op0=ALU.mult, op1=ALU.mult), ["W2", "W1"], [("qh", p)])
                yield

            s0_pref = {}
            sb_pref = {}
            sbrr = [0]

            def stage_b(s, h, g):
                p = g % 2
                ws = g % NWI
                nch = 11
                npc = 11 if s < 2 else 10
                khp, qhp = kh[p], qh[p]
                kk, qk = ("kh", p), ("qh", p)
                for (ca, cb) in ((0, 8), (8, 11)):
                    pj = alloc_pp()
                    for c in range(ca, cb):
                        for kc in range(8):
                            pg.add("pe", lambda e: e.matmul(out=pp[pj][0:64, (c - ca) * 128:(c - ca + 1) * 128],
                                                            lhsT=hT[:, kc, c * 64:(c + 1) * 64], rhs=win[ws][:, 2, kc, :],
                                                            start=(kc == 0), stop=(kc == 7)),
                                   [("win", ws), "hT"], [("pp", pj)], sig=(kc == 7 and c == cb - 1))
                    yield
                    if ca == 0:
                        pg.add("act", lambda e: e.activation(out=vt[0:64, ca:cb, :],
                                                             in_=pp[pj][0:64, 0:(cb - ca) * 128].rearrange("p (c e) -> p c e", e=128),
                                                             func=AF.Copy), [("pp", pj)], ["vt"])
                    else:
                        pg.add("dve", lambda e: e.tensor_copy(out=vt[0:64, ca:cb, :],
                                                              in_=pp[pj][0:64, 0:(cb - ca) * 128].rearrange("p (c e) -> p c e", e=128)),
                               [("pp", pj)], ["vt"])
                    held.discard(pj)
                    yield
                pj = alloc_pp()
                for c in range(nch):
                    pg.add("pe", lambda e: e.transpose(out=ppb[pj][0:64, c * 128:(c + 1) * 128], in_=khp[:, c * 64:(c + 1) * 64],
                                                       identity=ident_b[:, :]), [kk, "ident_b"], [("pp", pj)], sig=(c == nch - 1))
                yield
                pg.add("dve", lambda e: e.tensor_copy(out=khT[0:64, :, :], in_=ppb[pj][0:64, 0:nch * 128].rearrange("p (c e) -> p c e", e=128)),
                       [("pp", pj)], ["khT"])
                held.discard(pj)
                yield
                pj = alloc_pp()
                for c in range(nch):
                    pg.add("pe", lambda e: e.matmul(out=pp[pj][0:64, c * 64:(c + 1) * 64], lhsT=khp[:, c * 64:(c + 1) * 64],
                                                    rhs=qhp[:, c * 64:(c + 1) * 64], start=True, stop=True),
                           [kk, qk], [("pp", pj)], sig=(c == nch - 1))
                yield
                pg.add("dve", lambda e: e.tensor_tensor(
                    out=AT[0:64, 0:npc, :], in0=pp[pj][0:64, 0:npc * 64].rearrange("p (c t) -> p c t", t=64),
                    in1=bcast(tri[:, :].rearrange("p (o t) -> p o t", o=1), 1, npc), op=ALU.mult),
                    [("pp", pj), "tri"], ["AT"])
                if s == 2:
                    pg.add("dve", lambda e: e.tensor_tensor(out=AT[0:64, 10, :], in0=pp[pj][0:64, 640:704], in1=blk[:, :], op=ALU.mult),
                           [("pp", pj), "blk"], ["AT"])
                held.discard(pj)
                yield
                pg.add("act", lambda e: e.activation(out=Zd[:, :, 0], in_=Scar[:, h, :], func=AF.Copy), ["Scar"], ["Zd"])
                for (ca, cb) in ((0, 8), (8, npc)):
                    pj = alloc_pp()
                    for c in range(ca, cb):
                        pg.add("pe", lambda e: e.matmul(out=pp[pj][:, (c - ca) * 128:(c - ca + 1) * 128], lhsT=khT[:, c, :],
                                                        rhs=vt[:, c, :], start=True, stop=True),
                               ["khT", "vt"], [("pp", pj)], sig=(c == cb - 1))
                    yield
                    pg.add("dve", lambda e: e.tensor_tensor(
                        out=Zd[:, :, 1 + ca:1 + cb].rearrange("p e c -> p c e"),
                        in0=pp[pj][:, 0:(cb - ca) * 128].rearrange("p (c e) -> p c e", e=128),
                        in1=bcast(ebl[p][:, 1 + ca:1 + cb].rearrange("p (c o) -> p c o", o=1), 2, 128), op=ALU.mult),
                        [("pp", pj), ("ebl", p)], ["Zd"])
                    held.discard(pj)
                    yield
                pjg = proj(ws, 3)
                yield
                pg.add("dve", lambda e: e.tensor_copy(
                    out=D0[:, :, :], in_=bcast(ebl[p][:, 0:12].rearrange("p (o c) -> p o c", o=1), 1, 128)),
                    [("ebl", p)], ["D0"])
                yield
                pg.add("dve", lambda e: e.tensor_tensor_scan(
                    out=Sall.rearrange("p e c -> p (e c)"), data0=D0.rearrange("p e c -> p (e c)"),
                    data1=Zd.rearrange("p e c -> p (e c)"), initial=0.0, op0=ALU.mult, op1=ALU.add),
                    ["D0", "Zd"], ["Sall"])
                yield
                pg.add("act", lambda e: e.activation(out=Sbf[:, 0:6, :], in_=Sall[:, :, 0:6].rearrange("p e c -> p c e"),
                                                     func=AF.Copy), ["Sall"], [("Sbf", 0)])
                pg.add("dve", lambda e: e.tensor_copy(out=Sbf[:, 6:npc, :], in_=Sall[:, :, 6:npc].rearrange("p e c -> p c e")),
                       ["Sall"], [("Sbf", 1)])
                pg.add("act", lambda e: e.activation(out=Scar[:, h, :], in_=Sall[:, :, npc], func=AF.Copy), ["Sall"], ["Scar"])
                if s == 2:
                    pg.dma("sp", o_hgrn_p[h], Scar[:, h, :], ["Scar"], [], "ohp")
                yield
                po = alloc_pp()
                pok = ("pp", po)
                for c in range(npc):
                    pg.add("pe", lambda e: e.matmul(out=pp[po][:, c * 64:(c + 1) * 64], lhsT=vt[:, c, :], rhs=AT[:, c, :],
                                                    start=True, stop=False), ["vt", "AT"], [pok], sig=False)
                    pg.add("pe", lambda e: e.matmul(out=pp[po][:, c * 64:(c + 1) * 64], lhsT=Sbf[:, c, :], rhs=qhp[:, c * 64:(c + 1) * 64],
                                                    start=False, stop=True), [("Sbf", 0 if c < 6 else 1), qk], [pok], sig=(c == npc - 1))
                if s == 2:
                    pg.add("pe", lambda e: e.matmul(out=pp[po][:, 640:704], lhsT=vt[:, 10, :], rhs=AT[:, 10, :],
                                                    start=True, stop=False), ["vt", "AT"], [pok], sig=False)
                    def sb_load(hh, qg_):
                        pg.dma("pool", S0b[qg_][:], st_hgrn[4 * qg_:4 * qg_ + 4, hh].rearrange("s d e -> d s e"), [], [("S0b", qg_)], ("s0b", qg_))
                    if h == 0:
                        for qg in range(4):
                            sb_load(0, qg)
                    for qg in range(4):
                        i_ = qg
                        for sq_ in range(4):
                            sidx = 4 * qg + sq_
                            last = (qg == 3 and sq_ == 3)
                            pg.add("pe", lambda e: e.matmul(out=pp[po][:, 640 + 4 * sidx:644 + 4 * sidx], lhsT=S0b[i_][:, sq_, :],
                                                            rhs=qhp[:, 640 + 4 * sidx:644 + 4 * sidx], start=False, stop=last),
                                   [("S0b", i_), qk], [pok], sig=(sq_ == 3))
                        if h + 1 < 8:
                            sb_load(h + 1, qg)
                    yield
                yield
                pg.add("act", lambda e: e.activation(out=sqb, in_=pp[po][:, 0:TS], func=AF.Square), [pok], ["junk"])
                yield
                pn = alloc_pp()
                for (a, b_) in ((0, 512), (512, TS)):
                    pg.add("pe", lambda e: e.matmul(out=pp[pn][:, a:b_], lhsT=ones_b[:, :], rhs=sqb[:, a:b_], start=True, stop=True),
                           ["junk", "ones_b"], [("pp", pn)], sig=(a == 512))
                yield
                pg.add("act", lambda e: e.activation(out=W4[:], in_=pp[pn][:, 0:TS], func=AF.Ln, scale=1.0 / 128, bias=gn[:, 1:2]),
                       [("pp", pn), "gn"], ["W4"])
                held.discard(pn)
                yield
                pg.add("act", lambda e: e.activation(out=W4[:], in_=W4[:], func=AF.Exp, scale=-0.5), ["W4"], ["W4"])
                yield
                pg.add("dve", lambda e: e.scalar_tensor_tensor(out=W5[:], in0=pp[po][:, 0:TS], scalar=gn[:, 0:1], in1=W4[:],
                                                               op0=ALU.mult, op1=ALU.mult), [pok, "W4", "gn"], ["W5"])
                held.discard(po)
                yield
                pg.add("act", lambda e: e.activation(out=W4[:], in_=pp[pjg][:, 0:TS], func=AF.Silu), [("pp", pjg)], ["W4"])
                held.discard(pjg)
                yield
                pg.add("dve", lambda e: e.tensor_tensor(out=oT[:, h, :], in0=W5[:], in1=W4[:], op=ALU.mult), ["W5", "W4"], ["oT"])
                yield
                if s == 2:
                    def sq_load(hh, qg_):
                        i_ = s0rr[0] % 2
                        s0rr[0] += 1
                        pg.dma("sp", S0q[i_][:], st_hgrn[4 * qg_:4 * qg_ + 4, hh].rearrange("s d e -> d s e"), [], [("S0q", i_)], ("s0q", i_))
                        return i_
                    pg.add("dve", lambda e: e.tensor_tensor(
                        out=Vblk[:], in0=bcast(vt[:, 10:11, :], 1, 16),
                        in1=bcast(seqm[:, 0:16].rearrange("p (c o) -> p c o", o=1), 2, 128), op=ALU.mult),
                        ["vt", "seqm"], ["Vblk"])
                    nxt = s0_pref.pop(h, None)
                    if nxt is None:
                        nxt = sq_load(h, 0)
                    for qg in range(4):
                        i_ = nxt
                        if qg < 3:
                            nxt = sq_load(h, qg + 1)
                        elif h + 1 < 8:
                            s0_pref[h + 1] = sq_load(h + 1, 0)
                        pz = alloc_pp()
                        pg.add("pe", lambda e: e.matmul(out=pp[pz][:, 0:512], lhsT=khT[:, 10, :],
                                                        rhs=Vblk[:, 4 * qg:4 * qg + 4, :].rearrange("p c e -> p (c e)"), start=True, stop=True),
                               ["khT", "Vblk"], [("pp", pz)])
                        pg.add("dve", lambda e: e.tensor_tensor(out=tmpS[i_][:], in0=pp[pz][:, 0:512].rearrange("p (c e) -> p c e", e=128),
                                                                in1=S0q[i_][:], op=ALU.add), [("pp", pz), ("S0q", i_)], [("tmpS", i_)])
                        held.discard(pz)
                        pg.add("dve", lambda e: e.tensor_tensor(
                            out=tmpS[i_][:], in0=tmpS[i_][:], in1=bcast(ebs[p][:, 4 * qg:4 * qg + 4].rearrange("p (c o) -> p c o", o=1), 2, 128),
                            op=ALU.mult), [("tmpS", i_), ("ebs", p)], [("tmpS", i_)])
                        pg.dma("sp", o_hgrn_s[4 * qg:4 * qg + 4, h].rearrange("s d e -> d s e"), tmpS[i_][:], [("tmpS", i_)], [], ("ohs", i_))
                        yield

            def run_gens(gens, lead=0):
                gens = [g_ for g_ in gens if g_ is not None]
                for _ in range(lead):
                    try:
                        next(gens[0])
                    except StopIteration:
                        gens.pop(0)
                        break
                while gens:
                    for g_ in list(gens):
                        try:
                            next(g_)
                        except StopIteration:
                            gens.remove(g_)

            def prenorm_s(s):
                if s == 0:
                    prenorm_stats(0, 8)
                for (B, np_, b) in blocks(s):
                    hi_ = hb_rr[0] % 2
                    hb_rr[0] += 1
                    pg.add("dve", lambda e: e.scalar_tensor_tensor(
                        out=hb2[hi_][:np_, :], in0=X[:np_, B, :], scalar=stat[:np_, 8 + 6 * s + b:9 + 6 * s + b], in1=gb[0][:np_, :],
                        op0=ALU.mult, op1=ALU.mult), [("X", B), ("stat", 8 + 6 * s), ("gb", 0)], [("T2h", hi_)])
                    pj = next_pp()
                    for kc in range(8):
                        pg.add("pe", lambda e: e.transpose(
                            out=ppb[pj][:, kc * 128:kc * 128 + np_], in_=hb2[hi_][:np_, kc * 128:(kc + 1) * 128],
                            identity=ident_b[:np_, :np_]), [("T2h", hi_), "ident_b"], [("pp", pj)], sig=(kc == 7))
                    pg.add("act", lambda e: e.activation(
                        out=hT[:, :, b * 128:b * 128 + np_],
                        in_=ppb[pj][:, 0:1024].rearrange("p (c t) -> p c t", c=8)[:, :, 0:np_], func=AF.Copy),
                        [("pp", pj)], ["hT"])

            def outproj_gen(s):
                pg.dma("pool", wout, w_out.rearrange("(h e) n -> e h n", e=128), [], ZK + ["wout"], "wout")
                yield
                for (B, np_, b) in blocks(s):
                    pj = alloc_pp()
                    for h in range(8):
                        for half in range(2):
                            pg.add("pe", lambda e: e.matmul(out=pp[pj][:np_, half * 512:(half + 1) * 512], lhsT=oT[:, h, b * 128:b * 128 + np_],
                                                            rhs=wout[:, h, half * 512:(half + 1) * 512], start=(h == 0), stop=(h == 7)),
                                   ["oT", "wout"] + ZK, [("pp", pj)], sig=(h == 7 and half == 1))
                    yield
                    postnorm_residual(pj, B, np_, 1)
                    held.discard(pj)
                    yield

            for g0 in range(NWI):
                issue_win(g0)
            prenorm_s(0)
            run_gens([stage_a(0, 0, 0)])
            for s in range(NST):
                for h in range(8):
                    g = s * 8 + h
                    ga = stage_a(s, h + 1, g + 1) if h + 1 < 8 else None
                    if s + 1 < NST and h < 6:
                        stats_square(s + 1, 8 + 6 * (s + 1), h)
                    if s + 1 < NST and h == 6:
                        stats_final(8 + 6 * (s + 1))
                    run_gens([stage_b(s, h, g), ga], lead=HG_LEAD)
                    issue_win(g + NWI)
                if s + 1 < NST:
                    prenorm_s(s + 1)
                ga = stage_a(s + 1, 0, (s + 1) * 8) if s + 1 < NST else None
                run_gens([outproj_gen(s), ga])

    if "pool" in STAGES:
        pool_stage()
        pg.barrier()
    else:
        load_x()
    if "ffn0" in STAGES:
        ffn_stage(0)
        pg.barrier()
    if "hgrn" in STAGES:
        hgrn_stage()
        pg.barrier()
    if "ffn1" in STAGES:
        ffn_stage(1, final=True)
        pg.barrier()

    if "ffn1" not in STAGES:
        for s in range(NST):
            dst = yp[s * TS:s * TS + 640, :].rearrange("(b p) d -> p b d", p=128)
            pg.dma("sp", dst, X[:, s * 6:s * 6 + 5, :], [("X", s * 6 + b) for b in range(5)], [], ("xs", s))
            if s < 2:
                pg.dma("sp", yp[s * TS + 640:s * TS + 704, :], X[0:64, s * 6 + 5, :], [("X", s * 6 + 5)], [], ("xs5", s))
            else:
                pg.dma("sp", ys, X[0:64, 17, :], [("X", 17)], [], ("xs5", s))

    pg.emit()
    es.close()
    return nc


_CONSTS = None


def _consts():
    global _CONSTS
    if _CONSTS is None:
        ident = np.eye(128, dtype=np.float32)
        ti = np.arange(128)[:, None]
        to = np.arange(128)[None, :]
        band = np.zeros((128, 16, 128), np.float32)
        bs = np.zeros((128, 4, 24), np.float32)
        for g in range(4):
            w = 2 ** (g + 1)
            d = to - ti
            band[:, g, :] = ((d >= 0) & (d < w)) / w - (d == 0)
            dp = to + 128 - ti
            band[:, 4 + g, :] = ((dp > 0) & (dp < w)) / w
            d64 = to + 64 - ti
            band[:, 8 + g, :] = (((d64 > 0) & (d64 < w)) / w) * (ti < 64)
            cnt = np.minimum(to + 1, w)
            band[:, 12 + g, :] = ((d >= 0) & (d < w)) / cnt - (d == 0)
            for q in range(6):
                for r in range(19):
                    for t in range(4):
                        dd = 15 + t - r
                        bs[q * 19 + r, g, q * 4 + t] = (1.0 / w if 0 <= dd < w else 0.0) - (1.0 if dd == 0 else 0.0)
        tri = np.triu(np.ones((64, 64), np.float32))
        sid = np.arange(64) // 4
        blk = tri * (sid[:, None] == sid[None, :])
        seqm = (sid[:, None] == np.arange(16)[None, :]).astype(np.float32)
        rst = np.ones((128, 2, TS), np.float32)
        rst[:, :, ::64] = 0.0
        rst[:, 1, 640::4] = 0.0
        rst = rst.reshape(128, 2 * TS)
        _CONSTS = dict(c_ident=ident, c_band=band.reshape(128, 2048), c_bs=bs.reshape(128, 96), c_tri=tri, c_blk=blk.astype(np.float32),
                       c_seqm=seqm, c_rst=rst)
    return _CONSTS


_NC = None


def kernel(x_prompt, x_sample, state_pool, state_hgrn, state_ffn_conv, norm_mix_pre, norm_mix_post,
           norm_ffn_pre, norm_ffn_post, pool_w, pool_scale, hgrn_w_in, hgrn_lb_logits, hgrn_gnorm,
           hgrn_w_out, ffn_w_up, ffn_conv_w, ffn_conv_b, ffn_w_down):
    global _NC
    f = lambda a: np.ascontiguousarray(np.asarray(a, dtype=np.float32))
    x_prompt, x_sample, state_pool, state_hgrn, state_ffn_conv = map(f, (x_prompt, x_sample, state_pool, state_hgrn, state_ffn_conv))
    if _NC is None:
        _NC = build_program()
    nc = _NC
    shared = dict(
        n_mpre=f(norm_mix_pre), n_mpost=f(norm_mix_post), n_fpre=f(norm_ffn_pre), n_fpost=f(norm_ffn_post),
        pool_w=f(pool_w)[0], pool_scale=f(pool_scale), lb_logits=f(hgrn_lb_logits),
        w_in=np.ascontiguousarray(f(hgrn_w_in)[0].reshape(8, 128, 4, 8, 128).transpose(3, 1, 2, 0, 4)).reshape(8, 128, 4096),
        w_up=np.ascontiguousarray(f(ffn_w_up).reshape(2, 8, 128, 2, NJ, 128).transpose(0, 4, 2, 1, 3, 5)).reshape(2, NJ, 128, 2048),
        gnorm=f(hgrn_gnorm), w_out=f(hgrn_w_out)[0], conv_w=f(ffn_conv_w).reshape(6, 2 * DFF),
        conv_b=f(ffn_conv_b), w_down=f(ffn_w_down), **_consts())
    in_maps = []
    for i in range(NCORES):
        sl = slice(i * NSS, (i + 1) * NSS)
        m = dict(shared)
        m["xp"] = x_prompt[i]
        m["xs"] = x_sample[sl].reshape(64, D)
        m["st_pool"] = state_pool[0, sl]
        m["st_hgrn"] = state_hgrn[0, sl]
        m["st_ffn"] = state_ffn_conv[:, sl].reshape(2, NSS * 2, 2 * DFF)
        in_maps.append(m)
    res = run_bass_kernel_spmd(nc, in_maps, core_ids=list(range(NCORES)))
    R = res.results
    cat = lambda k: np.stack([np.asarray(r[k]) for r in R], 0)
    y_prompt = cat("yp")
    y_sample = cat("ys").reshape(128, 4, D)
    pool_p = cat("o_pool_p")[None]
    pool_s = cat("o_pool_s").reshape(1, 128, 15, D)
    hgrn_p = cat("o_hgrn_p")[None]
    hgrn_s = cat("o_hgrn_s").reshape(1, 128, 8, 128, 128)
    ffn_p = cat("o_ffn_p").transpose(1, 0, 2, 3)
    ffn_s = cat("o_ffn_s").reshape(8, 2, NSS, 2, 2 * DFF).transpose(1, 0, 2, 3, 4).reshape(2, 128, 2, 2 * DFF)
    return (y_prompt, y_sample, pool_p, pool_s, hgrn_p, hgrn_s, ffn_p, ffn_s)
```

```python
import os
import numpy as np
from contextlib import ExitStack
import concourse.bass as bass
import concourse.mybir as mybir
from concourse.bass_utils import run_bass_kernel_spmd

F32, BF16 = mybir.dt.float32, mybir.dt.bfloat16
AF = mybir.ActivationFunctionType
ALU = mybir.AluOpType

NCORES = 8
D = 1024
SEQ = 2048
NSS = 16
DFF = 2816
NJ = 22
TS = 704
NST = 3
EPS = 1e-6
STAGES = os.environ.get("MK_STAGES", "pool,ffn0,hgrn,ffn1").split(",")
VAL_POOL = os.environ.get("VAL_POOL", "0") == "1"
PROD_ENG = os.environ.get("PROD_ENG", "dve")
HG_LEAD = int(os.environ.get("HG_LEAD", "12"))
HG_SKIP = os.environ.get("HG_SKIP", "").split(",")


def bcast(ap, axis, n):
    l = [list(x) for x in ap.ap]
    assert l[axis][1] == 1, (l, axis)
    l[axis] = [0, n]
    return bass.AP(tensor=ap.tensor, offset=ap.offset, ap=l)


class _Rec:
    def __getattr__(self, name):
        def f(*a, **kw):
            self.call = (name, a, kw)
            return self
        return f


class Op:
    __slots__ = ("eng", "fn", "deps", "is_dma", "sem", "semval", "sig", "pos", "sigval")


class Prog:
    ENGS = ("pe", "act", "dve", "pool", "sp")

    def __init__(self, nc, es):
        self.nc = nc
        self.es = es
        self.ops = {e: [] for e in self.ENGS}
        self.res = {}
        self.dma_sems = {}
        self.pending_dmas = []
        self.bar = {}
        self.free_sems = {}
        self.sem_eng = {}
        self.all_sems = []
        self.esem = {e: es.enter_context(nc.semaphore("sem_" + e)) for e in ("pe", "act", "dve", "pool")}

    def add(self, eng, fn, reads=(), writes=(), sig=True, dma_key=None):
        op = Op()
        rec = _Rec()
        fn(rec)
        name_, a_, kw_ = rec.call
        fn = lambda e, name_=name_, a_=a_, kw_=kw_: getattr(e, name_)(*a_, **kw_)
        op.eng, op.fn, op.sig, op.is_dma = eng, fn, sig, dma_key is not None
        op.deps = []
        op.sem = None
        op.semval = 0
        op.sigval = 0
        if self.bar.get(eng):
            op.deps += [(d, "bar") for d in self.bar.pop(eng)]
        for k in reads:
            st = self.res.setdefault(k, [None, []])
            if st[0] is not None:
                op.deps.append((st[0], "raw"))
            st[1].append(op)
        for k in writes:
            st = self.res.setdefault(k, [None, []])
            if st[0] is not None and st[0] is not op:
                op.deps.append((st[0], "waw"))
            for r in st[1]:
                if r is not op:
                    op.deps.append((r, "war"))
            st[0] = op
            st[1] = []
        if op.is_dma:
            if dma_key not in self.dma_sems:
                fl = self.free_sems.setdefault(eng, [])
                if fl:
                    self.dma_sems[dma_key] = fl.pop()
                else:
                    ent = [self.es.enter_context(self.nc.semaphore("dq%d" % len(self.all_sems))), 0]
                    self.all_sems.append(ent)
                    self.dma_sems[dma_key] = ent
            ent = self.dma_sems[dma_key]
            self.sem_eng[id(ent)] = eng
            ent[1] += 16
            op.sem, op.semval = ent[0], ent[1]
            self.pending_dmas.append(op)
        op.pos = len(self.ops[eng])
        self.ops[eng].append(op)
        return op

    def barrier(self):
        deps = list(self.pending_dmas)
        self.pending_dmas = []
        for e in ("pe", "act", "dve", "pool"):
            for op in reversed(self.ops[e]):
                if not op.is_dma:
                    deps.append(op)
                    break
        self.bar = {e: list(deps) for e in self.ENGS}
        for ent in self.dma_sems.values():
            self.free_sems.setdefault(self.sem_eng[id(ent)], []).append(ent)
        self.dma_sems = {}

    def dma(self, eng, out, in_, reads, writes, key):
        return self.add(eng, lambda e: e.dma_start(out=out, in_=in_), reads, writes, dma_key=key)

    def op(self, eng, meth, reads, writes, sig=True, **kw):
        return self.add(eng, lambda e: getattr(e, meth)(**kw), reads, writes, sig=sig)

    def emit(self):
        nc = self.nc
        nextsig = {}
        for e in ("pe", "act", "dve", "pool"):
            c = 0
            for op in self.ops[e]:
                if not op.is_dma and op.sig:
                    c += 1
                    op.sigval = c
            nxt = None
            arr = [0] * len(self.ops[e])
            for i in range(len(self.ops[e]) - 1, -1, -1):
                op = self.ops[e][i]
                if not op.is_dma and op.sig:
                    nxt = op.sigval
                arr[i] = nxt
            nextsig[e] = arr

        def run(e, engobj):
            seen = {}
            for op in self.ops[e]:
                need = {}
                for d, kind in op.deps:
                    if d.is_dma:
                        sem, val = d.sem, d.semval
                    else:
                        if d.eng == e and not op.is_dma:
                            if e == "pe":
                                continue
                        sem, val = self.esem[d.eng], nextsig[d.eng][d.pos]
                        assert val is not None
                    if val > need.get(sem.num, (None, 0))[1]:
                        need[sem.num] = (sem, val)
                for num, (sem, val) in need.items():
                    if seen.get(num, 0) >= val:
                        continue
                    engobj.wait_ge(sem, val)
                    seen[num] = val
                inst = op.fn(engobj)
                if op.is_dma:
                    inst.then_inc(op.sem, 16)
                elif op.sig:
                    inst.then_inc(self.esem[e], 1)
            if e == "sp":
                for sem, cnt in self.all_sems:
                    if seen.get(sem.num, 0) < cnt:
                        engobj.wait_ge(sem, cnt)

        with nc.Block() as block:
            @block.tensor
            def _(t):
                run("pe", t)

            @block.scalar
            def _(a):
                run("act", a)

            @block.vector
            def _(v):
                run("dve", v)

            @block.gpsimd
            def _(g):
                run("pool", g)

            @block.sync
            def _(s):
                run("sp", s)


def build_program():
    nc = bass.Bass("TRN2", target_bir_lowering=False)
    es = ExitStack()
    dt_in = lambda name, shape: nc.dram_tensor(name, list(shape), F32, kind="ExternalInput").ap()
    dt_out = lambda name, shape: nc.dram_tensor(name, list(shape), F32, kind="ExternalOutput").ap()
    xp = dt_in("xp", (SEQ, D))
    xs = dt_in("xs", (64, D))
    st_pool = dt_in("st_pool", (NSS, 15, D))
    st_hgrn = dt_in("st_hgrn", (NSS, 8, 128, 128))
    st_ffn = dt_in("st_ffn", (2, NSS * 2, 2 * DFF))
    n_mpre = dt_in("n_mpre", (2, D))
    n_mpost = dt_in("n_mpost", (2, D))
    n_fpre = dt_in("n_fpre", (2, D))
    n_fpost = dt_in("n_fpost", (2, D))
    pool_w = dt_in("pool_w", (4, 256, 256))
    pool_scale = dt_in("pool_scale", (1, D))
    w_in = dt_in("w_in", (8, 128, 4096))
    lb_logits = dt_in("lb_logits", (2, D))
    gnorm = dt_in("gnorm", (1, 128))
    w_out = dt_in("w_out", (D, D))
    w_up = dt_in("w_up", (2, NJ, 128, 2048))
    conv_w = dt_in("conv_w", (6, 2 * DFF))
    conv_b = dt_in("conv_b", (2, 2 * DFF))
    w_down = dt_in("w_down", (2, DFF, D))
    c_ident = dt_in("c_ident", (128, 128))
    c_band = dt_in("c_band", (128, 16 * 128))
    c_bs = dt_in("c_bs", (128, 4 * 24))
    c_tri = dt_in("c_tri", (64, 64))
    c_blk = dt_in("c_blk", (64, 64))
    c_seqm = dt_in("c_seqm", (64, 16))
    c_rst = dt_in("c_rst", (128, 2 * TS))

    yp = dt_out("yp", (SEQ, D))
    ys = dt_out("ys", (64, D))
    o_pool_p = dt_out("o_pool_p", (15, D))
    o_pool_s = dt_out("o_pool_s", (NSS, 15, D))
    o_hgrn_p = dt_out("o_hgrn_p", (8, 128, 128))
    o_hgrn_s = dt_out("o_hgrn_s", (NSS, 8, 128, 128))
    o_ffn_p = dt_out("o_ffn_p", (2, 2, 2 * DFF))
    o_ffn_s = dt_out("o_ffn_s", (2, NSS * 2, 2 * DFF))

    pg = Prog(nc, es)
    sb = lambda name, shape, dt=F32: es.enter_context(nc.sbuf_tensor(name, list(shape), dt))
    X = sb("X", (128, 18, D))
    ident_f = sb("ident_f", (128, 128))
    ident_b = sb("ident_b", (128, 128), BF16)
    gb = [sb("gb%d" % i, (128, D)) for i in range(2)]
    T2 = sb("T2", (128, D))
    T2b = T2.bitcast(BF16)
    hb2 = [T2b[:, 0:D], T2b[:, D:2 * D]]
    T2K = ["T2", ("T2h", 0), ("T2h", 1)]
    hb_rr = [0]
    junk = sb("junk", (128, D), BF16)
    stat = sb("stat", (128, 64))
    neghalf = sb("neghalf", (128, 8))
    pp = [es.enter_context(nc.psum_tensor("pp%d" % i, [128, 1024], F32)) for i in range(4)]
    ppb = [p.bitcast(BF16) for p in pp]

    blocks = lambda s: [(s * 6 + b, 128 if b < 5 else 64, b) for b in range(6)]

    pg.dma("sp", ident_f[:], c_ident, [], ["ident_f"], "c0")
    pg.dma("pool", ident_b[:], c_ident, [], ["ident_b"], "c1")
    pg.add("dve", lambda e: e.memset(neghalf[:], -0.5), [], ["neghalf"])
    pg.add("dve", lambda e: e.memset(stat[:], 1.0), [], [("stat", c) for c in (0, 6, 12, 8, 14, 20, 60)] + [("stat", c) for c in range(32, 48)])

    def load_x():
        for s in range(NST):
            for b in range(5):
                r0 = s * TS + b * 128
                if s == 0 and b == 0 and "pool" in STAGES:
                    continue
                pg.dma("sp", X[:, s * 6 + b, :], xp[r0:r0 + 128, :], [], [("X", s * 6 + b)], ("xl", s, b))
            if s < 2:
                pg.dma("sp", X[0:64, s * 6 + 5, :], xp[s * TS + 640:s * TS + 704, :], [], [("X", s * 6 + 5)], ("xl5", s))
            else:
                pg.dma("sp", X[0:64, 17, :], xs, [], [("X", 17)], ("xl5", s))


    def load_gb(slot, src, row):
        a = bass.AP(tensor=src.tensor, offset=row * D, ap=[[0, 128], [1, D]])
        pg.dma("sp", gb[slot][:], a, [], [("gb", slot)], ("gb", slot))

    uid_c = [0]

    def uid():
        uid_c[0] += 1
        return uid_c[0]

    pp_rr = [0]

    def next_pp():
        i = pp_rr[0] % 4
        pp_rr[0] += 1
        return i

    st_rr = [0]

    def stats_square(s, col0, bi):
        (B, np_, b) = blocks(s)[bi]
        pg.add("act", lambda e: e.activation(
            out=junk[:np_, :], in_=X[:np_, B, :], func=AF.Square, accum_out=stat[:np_, col0 + b:col0 + b + 1]),
            [("X", B)], ["junk", ("stat", col0)])

    def stats_final(col0, n=6):
        pg.add("dve", lambda e: e.tensor_scalar(out=stat[:, col0:col0 + n], in0=stat[:, col0:col0 + n],
                                                scalar1=1.0 / D, scalar2=EPS, op0=ALU.mult, op1=ALU.add),
               [("stat", col0)], [("stat", col0)])
        pg.add("pool", lambda e: e.tensor_tensor(out=stat[:, col0:col0 + n], in0=stat[:, col0:col0 + n],
                                                 in1=neghalf[:, 0:n], op=ALU.pow),
               [("stat", col0), "neghalf"], [("stat", col0)])

    def prenorm_stats(s, col0):
        for (B, np_, b) in blocks(s):
            pg.add("act", lambda e, B=B, np_=np_, b=b: e.activation(
                out=junk[:np_, :], in_=X[:np_, B, :], func=AF.Square, accum_out=stat[:np_, col0 + b:col0 + b + 1]),
                [("X", B)], ["junk", ("stat", col0)])
        pg.add("dve", lambda e: e.tensor_scalar(out=stat[:, col0:col0 + 6], in0=stat[:, col0:col0 + 6],
                                                scalar1=1.0 / D, scalar2=EPS, op0=ALU.mult, op1=ALU.add),
               [("stat", col0)], [("stat", col0)])
        pg.add("pool", lambda e: e.tensor_tensor(out=stat[:, col0:col0 + 6], in0=stat[:, col0:col0 + 6],
                                                 in1=neghalf[:, 0:6], op=ALU.pow),
               [("stat", col0), "neghalf"], [("stat", col0)])

    def postnorm_residual(ppi, B, np_, gslot, scale_slot=None, T1=None):
        c = 32 + (st_rr[0] % 16)
        st_rr[0] += 1
        src = pp[ppi][:np_, :]
        srck = ("pp", ppi)
        if scale_slot is not None:
            pg.add("dve", lambda e: e.tensor_tensor(out=T1[:np_, :], in0=src, in1=gb[scale_slot][:np_, :], op=ALU.mult),
                   [srck, ("gb", scale_slot)], ["T1"])
            src = T1[:np_, :]
            srck = "T1"
        pg.add("act", lambda e: e.activation(out=junk[:np_, :], in_=src, func=AF.Square,
                                             accum_out=stat[:np_, c:c + 1]), [srck], ["junk", ("stat", c)])
        pg.add("dve", lambda e: e.tensor_scalar(out=stat[:np_, c:c + 1], in0=stat[:np_, c:c + 1],
                                                scalar1=1.0 / D, scalar2=EPS, op0=ALU.mult, op1=ALU.add),
               [("stat", c)], [("stat", c)])
        pg.add("pool", lambda e: e.tensor_tensor(out=stat[:np_, c:c + 1], in0=stat[:np_, c:c + 1],
                                                 in1=neghalf[:np_, 0:1], op=ALU.pow),
               [("stat", c), "neghalf"], [("stat", c)])
        pg.add("dve", lambda e: e.scalar_tensor_tensor(out=T2[:np_, :], in0=src, scalar=stat[:np_, c:c + 1],
                                                       in1=gb[gslot][:np_, :], op0=ALU.mult, op1=ALU.mult),
               [srck, ("stat", c), ("gb", gslot)], T2K)
        pg.add("pool", lambda e: e.tensor_tensor(out=X[:np_, B, :], in0=X[:np_, B, :], in1=T2[:np_, :], op=ALU.add),
               [("X", B)] + T2K, [("X", B)])

    def pool_stage():
        with ExitStack() as ps:
            sbp = lambda name, shape, dt=F32: ps.enter_context(nc.sbuf_tensor("%s_%d" % (name, uid()), list(shape), dt))
            PT = sbp("PT", (128, 8, TS), BF16)
            H32 = [sbp("H32_%d" % i, (128, D)) for i in range(2)]
            Hhi = [sbp("Hhi%d" % i, (128, D), BF16) for i in range(2)]
            Hlo = [sbp("Hlo%d" % i, (128, D), BF16) for i in range(2)]
            FULLS = [sbp("FULL%d" % i, (128, D)) for i in range(3)]
            H32s = sbp("H32s", (64, D))
            T1 = sbp("T1", (128, D))
            wpool = sbp("wpool", (128, 4, 2, 256), BF16)
            BND = sbp("BND", (128, 16, 128), BF16)
            BS = sbp("BS", (128, 4, 24), BF16)
            gbs = sbp("gbs", (128, D))
            pg.dma("sp", X[:, 0, :], xp[0:128, :], [], [("X", 0)], ("xl", 0, 0))
            load_gb(0, n_mpre, 0)
            load_gb(1, n_mpost, 0)
            pg.dma("pool", wpool[:], pool_w.rearrange("g (dc dp) e -> dp g dc e", dp=128), [], ["wpool"], "wpool")
            pg.dma("pool", BND[:], c_band.rearrange("p (m t) -> p m t", t=128), [], ["BND"], "bnd")
            pg.dma("pool", BS[:], c_bs.rearrange("p (m t) -> p m t", t=24), [], ["BS"], "bs")
            pg.dma("sp", gbs[:], bass.AP(tensor=pool_scale.tensor, offset=0, ap=[[0, 128], [1, D]]), [], ["gbs"], "gbs")
            load_x()
            pg.dma("sp", o_pool_s[:, 0:11, :], st_pool[:, 4:15, :], [], [], "poolctx")
            hrr = [0]

            class Back:
                def __init__(self, B, np_, tok0):
                    self.B, self.np_, self.tok0 = B, np_, tok0
                    self.c = 32 + (st_rr[0] % 16)
                    st_rr[0] += 1

                def mm(self):
                    self.ppi = next_pp()
                    ppi, np_, tok0 = self.ppi, self.np_, self.tok0
                    for g in range(4):
                        for dc in range(2):
                            pg.add("pe", lambda e: e.matmul(
                                out=pp[ppi][:np_, g * 256:(g + 1) * 256], lhsT=PT[:, 2 * g + dc, tok0:tok0 + np_],
                                rhs=wpool[:, g, dc, :], start=(dc == 0), stop=(dc == 1)),
                                ["PT", "wpool"], [("pp", ppi)], sig=(g == 3 and dc == 1))

                def t1(self):
                    ppi, np_ = self.ppi, self.np_
                    pg.add("dve", lambda e: e.tensor_tensor(out=T1[:np_, :], in0=pp[ppi][:np_, :], in1=gbs[:np_, :], op=ALU.mult),
                           [("pp", ppi), "gbs"], ["T1"])

                def sq(self):
                    np_, c = self.np_, self.c
                    pg.add("act", lambda e: e.activation(out=junk[:np_, :], in_=T1[:np_, :], func=AF.Square,
                                                         accum_out=stat[:np_, c:c + 1]), ["T1"], ["junk", ("stat", c)])

                def st(self):
                    np_, c = self.np_, self.c
                    pg.add("dve", lambda e: e.tensor_scalar(out=stat[:np_, c:c + 1], in0=stat[:np_, c:c + 1],
                                                            scalar1=1.0 / D, scalar2=EPS, op0=ALU.mult, op1=ALU.add),
                           [("stat", c)], [("stat", c)])
                    pg.add("pool", lambda e: e.tensor_tensor(out=stat[:np_, c:c + 1], in0=stat[:np_, c:c + 1],
                                                             in1=neghalf[:np_, 0:1], op=ALU.pow),
                           [("stat", c), "neghalf"], [("stat", c)])

                def t2(self):
                    np_, c, B = self.np_, self.c, self.B
                    pg.add("dve", lambda e: e.scalar_tensor_tensor(out=T2[:np_, :], in0=T1[:np_, :], scalar=stat[:np_, c:c + 1],
                                                                   in1=gb[1][:np_, :], op0=ALU.mult, op1=ALU.mult),
                           ["T1", ("stat", c), ("gb", 1)], T2K)
                    pg.add("pool", lambda e: e.tensor_tensor(out=X[:np_, B, :], in0=X[:np_, B, :], in1=T2[:np_, :], op=ALU.add),
                           [("X", B)] + T2K, [("X", B)])

                def all(self):
                    self.mm(); self.t1(); self.sq(); self.st(); self.t2()

            class Src:
                def __init__(self, t, key):
                    self.t, self.name_key = t, key

                def __getitem__(self, idx):
                    return self.t[idx]

            def hi(src, hs, rows):
                pg.add("act", lambda e: e.activation(out=Hhi[hs][:rows, :], in_=src[:rows, :], func=AF.Copy), src.name_key, [("Hhi", hs)])

            def lo(src, hs, rows):
                pg.add("dve", lambda e: e.tensor_tensor(out=Hlo[hs][:rows, :], in0=src[:rows, :], in1=Hhi[hs][:rows, :], op=ALU.subtract),
                       src.name_key + [("Hhi", hs)], [("Hlo", hs)])

            prev = None
            back = None
            for s in range(NST):
                if s > 0 and back is not None:
                    back.all()
                    back = None
                if s == 0:
                    prenorm_stats(0, 0)
                    for kb in range(3):
                        pg.add("dve", lambda e: e.memset(FULLS[kb][:], 0.0), [], [("FULL", kb, i) for i in range(12)])
                if s == 1:
                    pg.add("act", lambda e: e.activation(out=junk[:64, :], in_=X[:64, 17, :], func=AF.Square,
                                                         accum_out=stat[:64, 60:61]), [("X", 17)], ["junk", ("stat", 60)])
                    stats_final(60, 1)
                    pg.add("dve", lambda e: e.scalar_tensor_tensor(
                        out=H32s[:64, :], in0=X[:64, 17, :], scalar=stat[:64, 60:61], in1=gb[0][:64, :],
                        op0=ALU.mult, op1=ALU.mult), [("X", 17), ("stat", 60), ("gb", 0)], ["H32s"])
                    for q in range(NSS):
                        pg.dma("sp", o_pool_s[q, 11:15, :], H32s[4 * q:4 * q + 4, :], ["H32s"], [], ("ops", q % 4))
                    for kb, (q0, nsq) in enumerate(((0, 6), (6, 6), (12, 4))):
                        for ql in range(nsq):
                            pg.dma("sp", FULLS[kb][ql * 19:ql * 19 + 15, :], st_pool[q0 + ql], [], [("FULL", kb, 2 * ql)], ("fl", kb))
                            pg.dma("sp", FULLS[kb][ql * 19 + 15:ql * 19 + 19, :], H32s[4 * (q0 + ql):4 * (q0 + ql) + 4, :],
                                   ["H32s"], [("FULL", kb, 2 * ql + 1)], ("fl", kb))
                for (B, np_, b) in blocks(s):
                    hs = hrr[0] % 2
                    hrr[0] += 1
                    sample = (s == 2 and b == 5)
                    if s + 1 < NST:
                        stats_square(s + 1, 6 * (s + 1), b)
                        if b == 5:
                            stats_final(6 * (s + 1))
                    if sample:
                        if back is not None:
                            back.all()
                            back = None
                        for kb, (q0, nsq) in enumerate(((0, 6), (6, 6), (12, 4))):
                            fs = hrr[0] % 2
                            hrr[0] += 1
                            fkeys = [("FULL", kb, i) for i in range(12)]
                            hi(Src(FULLS[kb], fkeys), fs, 128)
                            lo(Src(FULLS[kb], fkeys), fs, 128)
                            pj = next_pp()
                            for c in range(8):
                                g = c // 2
                                for ti_, Ht in enumerate((Hhi[fs], Hlo[fs])):
                                    pg.add("pe", lambda e: e.matmul(
                                        out=pp[pj][:, c * 128:c * 128 + nsq * 4], lhsT=Ht[:, c * 128:(c + 1) * 128],
                                        rhs=BS[:, g, 0:nsq * 4], start=(ti_ == 0), stop=(ti_ == 1)),
                                        [("Hhi", fs), ("Hlo", fs), "BS"], [("pp", pj)], sig=(c == 7 and ti_ == 1))
                            pg.add("act", lambda e: e.activation(
                                out=PT[:, :, 640 + q0 * 4:640 + (q0 + nsq) * 4],
                                in_=pp[pj][:, :].rearrange("p (c t) -> p c t", c=8)[:, :, 0:nsq * 4], func=AF.Copy),
                                [("pp", pj)], ["PT"])
                        Back(B, np_, 640).all()
                        continue
                    pg.add("dve", lambda e: e.scalar_tensor_tensor(
                        out=H32[hs][:np_, :], in0=X[:np_, B, :], scalar=stat[:np_, 6 * s + b:6 * s + b + 1], in1=gb[0][:np_, :],
                        op0=ALU.mult, op1=ALU.mult), [("X", B), ("stat", 6 * s), ("gb", 0)], [("H32", hs)])
                    if s == 2 and b == 4:
                        pg.dma("sp", o_pool_p, H32[hs][113:128, :], [("H32", hs)], [], ("opp", hs))
                    if np_ == 64:
                        pg.add("dve", lambda e: e.memset(Hhi[hs][64:128, :], 0.0), [], [("Hhi", hs)])
                        pg.add("dve", lambda e: e.memset(Hlo[hs][64:128, :], 0.0), [], [("Hlo", hs)])
                    hsrc = Src(H32[hs], [("H32", hs)])
                    hi(hsrc, hs, np_)
                    if back is not None:
                        back.mm()
                        back.t1()
                        back.sq()
                    lo(hsrc, hs, np_)
                    if s == 0 and b == 0:
                        terms = [(Hhi[hs], ("Hhi", hs), 12), (Hlo[hs], ("Hlo", hs), 12)]
                    else:
                        phs, pnp = prev
                        pm = 4 if pnp == 128 else 8
                        terms = [(Hhi[hs], ("Hhi", hs), 0), (Hlo[hs], ("Hlo", hs), 0),
                                 (Hhi[phs], ("Hhi", phs), pm), (Hlo[phs], ("Hlo", phs), pm)]
                    pj = next_pp()
                    for c in range(8):
                        g = c // 2
                        for ti_, (Ht, hk, mbase) in enumerate(terms):
                            pg.add("pe", lambda e: e.matmul(
                                out=pp[pj][:, c * 128:c * 128 + np_], lhsT=Ht[:, c * 128:(c + 1) * 128],
                                rhs=BND[:, mbase + g, 0:np_], start=(ti_ == 0), stop=(ti_ == len(terms) - 1)),
                                [hk, "BND"], [("pp", pj)], sig=(c == 7 and ti_ == len(terms) - 1))
                    if back is not None:
                        back.st()
                    pg.add("act", lambda e: e.activation(
                        out=PT[:, :, b * 128:b * 128 + np_],
                        in_=pp[pj][:, :].rearrange("p (c t) -> p c t", c=8)[:, :, 0:np_], func=AF.Copy),
                        [("pp", pj)], ["PT"])
                    if back is not None:
                        back.t2()
                    back = Back(B, np_, b * 128)
                    prev = (hs, np_)

    def store_x_block(s, B, np_, b):
        if b < 5:
            r0 = s * TS + b * 128
            pg.dma("sp", yp[r0:r0 + 128, :], X[:, B, :], [("X", B)], [], ("xs", B % 4))
        elif s < 2:
            pg.dma("sp", yp[s * TS + 640:s * TS + 704, :], X[0:64, B, :], [("X", B)], [], ("xs", B % 4))
        else:
            pg.dma("sp", ys, X[0:64, 17, :], [("X", 17)], [], ("xs", B % 4))

    def ffn_stage(li, final=False):
        with ExitStack() as ps:
            sbp = lambda name, shape, dt=F32: ps.enter_context(nc.sbuf_tensor("%s_%d" % (name, uid()), list(shape), dt))
            hT = sbp("hT", (128, 8, TS + 2), BF16)
            GT = sbp("GT", (128, NJ, TS), BF16)
            NW = 3
            wup = [sbp("wup%d" % i, (128, 8, 2, 128), BF16) for i in range(NW)]
            wdn = sbp("wdn", (128, NJ, D), BF16)
            Tg = [sbp("Tg%d" % i, (128, 512)) for i in range(3)]
            Tv = [sbp("Tv%d" % i, (128, 512)) for i in range(3)]
            Tx = [sbp("Tx%d" % i, (128, 512)) for i in range(2 if (VAL_POOL or PROD_ENG == "pool32") else 0)]
            hbf = [junk]
            cwT = sbp("cwT", (128, 44, 4))
            SU = sbp("SU", (128, 44, 34))
            FU = [sbp("FU%d" % i, (128, 16, 6)) for i in range(2)]
            load_gb(0, n_fpre, li)
            load_gb(1, n_fpost, li)
            stgf = wdn.bitcast(F32).rearrange("p j n -> p (j n)")
            wk = [("wdn", j_) for j_ in range(NJ)]
            pg.dma("sp", stgf[0:3, 0:2 * DFF], conv_w[li * 3:li * 3 + 3, :], [], wk[:11], "stga")
            pg.dma("sp", stgf[3:4, 0:2 * DFF], conv_b[li:li + 1, :], [], wk[:11], "stgb")
            pg.dma("sp", stgf[0:32, 2 * DFF:4 * DFF], st_ffn[li, :, :], [], wk[11:], "stgc")
            prenorm_stats(0, 8)
            trr = [0]
            frr = [0]
            NT = len(Tg)
            pairs = [(s, j) for s in range(NST) for j in range(NJ)]
            def issue_w(k):
                if k >= len(pairs):
                    return
                s_, j_ = pairs[k]
                ws_ = k % NW
                pg.dma("pool", wup[ws_][:].rearrange("p a b c -> p (a b c)"), w_up[li, j_], [], [("wup", ws_)], ("wup", ws_, 0))

            def prenorm(s):
                ncols_ = TS + 2
                if s == 0:
                    pg.add("dve", lambda e: e.memset(hT[:, :, 0:2], 0.0), [], ["hT"])
                else:
                    pg.add("dve", lambda e: e.tensor_copy(out=hT[:, :, 0:2], in_=hT[:, :, TS:TS + 2]), ["hT"], ["hT"])
                for (B, np_, b) in blocks(s):
                    hi_ = hb_rr[0] % 2
                    hb_rr[0] += 1
                    pg.add("dve", lambda e: e.scalar_tensor_tensor(
                        out=hb2[hi_][:np_, :], in0=X[:np_, B, :], scalar=stat[:np_, 8 + 6 * s + b:9 + 6 * s + b], in1=gb[0][:np_, :],
                        op0=ALU.mult, op1=ALU.mult), [("X", B), ("stat", 8 + 6 * s), ("gb", 0)], [("T2h", hi_)])
                    pj = next_pp()
                    for kc in range(8):
                        pg.add("pe", lambda e: e.transpose(
                            out=ppb[pj][:, kc * 128:kc * 128 + np_], in_=hb2[hi_][:np_, kc * 128:(kc + 1) * 128],
                            identity=ident_b[:np_, :np_]), [("T2h", hi_), "ident_b"], [("pp", pj)], sig=(kc == 7))
                    pg.add("act", lambda e: e.activation(
                        out=hT[:, :, 2 + b * 128:2 + b * 128 + np_],
                        in_=ppb[pj][:, 0:1024].rearrange("p (c t) -> p c t", c=8)[:, :, 0:np_], func=AF.Copy),
                        [("pp", pj)], ["hT"])

            def conv_tile(s, j, ws, c0, c1, is_last_prompt):
                n = c1 - c0
                m = n - 2
                pj = next_pp()
                for half in range(2):
                    for kc in range(8):
                        pg.add("pe", lambda e: e.matmul(
                            out=pp[pj][:, half * 512:half * 512 + n], lhsT=wup[ws][:, kc, half, :],
                            rhs=hT[:, kc, c0:c1], start=(kc == 0), stop=(kc == 7)),
                            [("wup", ws), "hT"], [("pp", pj)], sig=(kc == 7 and half == 1))
                ti = trr[0] % NT
                trr[0] += 1
                hv = [(0, Tg[ti], ("Tg", ti), j), (1, Tv[ti], ("Tv", ti), NJ + j)]
                U = [pp[pj][:, half * 512:half * 512 + n] for half in range(2)]
                pk = ("pp", pj)
                for half, Tt, key, ch in hv:
                    pg.add("act", lambda e: e.activation(
                        out=Tt[:, 0:m], in_=U[half][:, 0:m], func=AF.Identity, scale=cwT[:, ch, 0:1], bias=cwT[:, ch, 3:4]),
                        [pk, "cwT"], [key])
                if VAL_POOL:
                    chv = NJ + j
                    pg.add("act", lambda e: e.activation(out=Tx[ti % 2][:, 0:m], in_=U[1][:, 1:m + 1], func=AF.Copy,
                                                         scale=cwT[:, chv, 1:2]), [pk, "cwT"], [("Tx", ti % 2)])
                    pg.add("dve", lambda e: e.scalar_tensor_tensor(
                        out=Tg[ti][:, 0:m], in0=U[0][:, 1:m + 1], scalar=cwT[:, j, 1:2], in1=Tg[ti][:, 0:m],
                        op0=ALU.mult, op1=ALU.add), [pk, "cwT", ("Tg", ti)], [("Tg", ti)])
                    pg.add("pool", lambda e: e.tensor_tensor(out=Tv[ti][:, 0:m], in0=Tv[ti][:, 0:m], in1=Tx[ti % 2][:, 0:m], op=ALU.add),
                           [("Tv", ti), ("Tx", ti % 2)], [("Tv", ti)])
                    for half, Tt, key, ch in hv:
                        pg.add("dve", lambda e: e.scalar_tensor_tensor(
                            out=Tt[:, 0:m], in0=U[half][:, 2:m + 2], scalar=cwT[:, ch, 2:3], in1=Tt[:, 0:m],
                            op0=ALU.mult, op1=ALU.add), [pk, "cwT", key], [key])
                else:
                    for tap in (1, 2):
                        for half, Tt, key, ch in hv:
                            pg.add("dve", lambda e: e.scalar_tensor_tensor(
                                out=Tt[:, 0:m], in0=U[half][:, tap:m + tap], scalar=cwT[:, ch, tap:tap + 1], in1=Tt[:, 0:m],
                                op0=ALU.mult, op1=ALU.add), [pk, "cwT", key], [key])
                if is_last_prompt:
                    for half, Tt, key, ch in hv:
                        pg.add("act", lambda e: e.activation(out=SU[:, ch, 32:34], in_=U[half][:, n - 2:n], func=AF.Copy),
                               [pk], [("SU", ch)])
                pg.add("act", lambda e: e.activation(out=Tg[ti][:, 0:m], in_=Tg[ti][:, 0:m], func=AF.Gelu_apprx_tanh),
                       [("Tg", ti)], [("Tg", ti)])
                if PROD_ENG == "pool32":
                    pg.add("pool", lambda e: e.tensor_tensor(out=Tx[ti % 2][:, 0:m], in0=Tg[ti][:, 0:m], in1=Tv[ti][:, 0:m], op=ALU.mult),
                           [("Tg", ti), ("Tv", ti)], [("Tx", ti % 2)])
                    pg.add("act", lambda e: e.activation(out=GT[:, j, c0:c0 + m], in_=Tx[ti % 2][:, 0:m], func=AF.Copy),
                           [("Tx", ti % 2)], [("GT", j)])
                else:
                    pg.add(PROD_ENG, lambda e: e.tensor_tensor(out=GT[:, j, c0:c0 + m], in0=Tg[ti][:, 0:m], in1=Tv[ti][:, 0:m], op=ALU.mult),
                           [("Tg", ti), ("Tv", ti)], [("GT", j)])

            def sample_tile(j, ws):
                pj = next_pp()
                for half in range(2):
                    for kc in range(8):
                        pg.add("pe", lambda e: e.matmul(
                            out=pp[pj][:, half * 512:half * 512 + 64], lhsT=wup[ws][:, kc, half, :],
                            rhs=hT[:, kc, 642:706], start=(kc == 0), stop=(kc == 7)),
                            [("wup", ws), "hT"], [("pp", pj)], sig=(kc == 7 and half == 1))
                ti = trr[0] % NT
                trr[0] += 1
                for half, Tt, key in ((0, Tg[ti], ("Tg", ti)), (1, Tv[ti], ("Tv", ti))):
                    ch = half * NJ + j
                    fi = frr[0] % 2
                    frr[0] += 1
                    Fu = FU[fi]
                    fk = ("FU", fi)
                    pg.add("act", lambda e: e.activation(
                        out=Fu[:, :, 0:2], in_=SU[:, ch, 0:32].rearrange("p (s r) -> p s r", r=2), func=AF.Copy),
                        [("SU", ch)], [fk])
                    pg.add("act", lambda e: e.activation(
                        out=Fu[:, :, 2:6], in_=pp[pj][:, half * 512:half * 512 + 64].rearrange("p (s t) -> p s t", t=4),
                        func=AF.Copy), [("pp", pj)], [fk])
                    To = Tt[:, 0:64].rearrange("p (s t) -> p s t", t=4)
                    pg.add("act", lambda e: e.activation(
                        out=To, in_=Fu[:, :, 0:4], func=AF.Identity, scale=cwT[:, ch, 0:1], bias=cwT[:, ch, 3:4]),
                        [fk, "cwT"], [key])
                    pg.add("dve", lambda e: e.scalar_tensor_tensor(
                        out=To, in0=Fu[:, :, 1:5], scalar=cwT[:, ch, 1:2], in1=To, op0=ALU.mult, op1=ALU.add),
                        [fk, "cwT", key], [key])
                    pg.add("dve", lambda e: e.scalar_tensor_tensor(
                        out=To, in0=Fu[:, :, 2:6], scalar=cwT[:, ch, 2:3], in1=To, op0=ALU.mult, op1=ALU.add),
                        [fk, "cwT", key], [key])
                    pg.add("act", lambda e: e.activation(
                        out=SU[:, ch, 0:32].rearrange("p (s r) -> p s r", r=2), in_=Fu[:, :, 4:6], func=AF.Copy),
                        [fk], [("SU", ch)])
                pg.add("act", lambda e: e.activation(out=Tg[ti][:, 0:64], in_=Tg[ti][:, 0:64], func=AF.Gelu_apprx_tanh),
                       [("Tg", ti)], [("Tg", ti)])
                pg.add("dve", lambda e: e.tensor_tensor(out=GT[:, j, 640:704], in0=Tg[ti][:, 0:64], in1=Tv[ti][:, 0:64], op=ALU.mult),
                       [("Tg", ti), ("Tv", ti)], [("GT", j)])

            def emit_ffn_rows():
                for g4 in range(11):
                    pj = next_pp()
                    for ci in range(4):
                        ch = g4 * 4 + ci
                        pg.add("pe", lambda e, ch=ch, ci=ci, pj=pj: e.transpose(
                            out=pp[pj][0:34, ci * 128:(ci + 1) * 128], in_=SU[:, ch, :], identity=ident_f[:, :]),
                            [("SU", ch), "ident_f"], [("pp", pj)], sig=(ci == 3))
                    oi = g4 % 2
                    pg.add("act", lambda e, oi=oi, pj=pj: e.activation(out=Tg[oi][0:34, :], in_=pp[pj][0:34, 0:512], func=AF.Copy),
                           [("pp", pj)], [("Tg", oi)])
                    pg.dma("sp", o_ffn_s[li, :, g4 * 512:(g4 + 1) * 512], Tg[oi][0:32, :], [("Tg", oi)], [], ("ofs", oi))
                    pg.dma("sp", o_ffn_p[li, :, g4 * 512:(g4 + 1) * 512], Tg[oi][32:34, :], [("Tg", oi)], [], ("ofp", oi))


            for k0 in range(NW - 1):
                issue_w(k0)
            prenorm(0)
            for q in range(11):
                pj = next_pp()
                for ci in range(4):
                    ch = q * 4 + ci
                    pg.add("pe", lambda e: e.transpose(out=pp[pj][:, ci * 4:ci * 4 + 4], in_=stgf[0:4, ch * 128:(ch + 1) * 128],
                                                       identity=ident_f[:4, :4]), [("wdn", q), "ident_f"], [("pp", pj)], sig=False)
                for ci in range(4):
                    ch = q * 4 + ci
                    pg.add("pe", lambda e: e.transpose(out=pp[pj][:, 512 + ci * 32:512 + (ci + 1) * 32],
                                                       in_=stgf[0:32, 2 * DFF + ch * 128:2 * DFF + (ch + 1) * 128],
                                                       identity=ident_f[:32, :32]), [("wdn", 11 + q), "ident_f"], [("pp", pj)], sig=(ci == 3))
                pg.add("act", lambda e: e.activation(out=cwT[:, q * 4:q * 4 + 4, :], in_=pp[pj][:, 0:16].rearrange("p (c k) -> p c k", k=4),
                                                     func=AF.Copy), [("pp", pj)], ["cwT"])
                pg.add("act", lambda e: e.activation(out=SU[:, q * 4:q * 4 + 4, 0:32],
                                                     in_=pp[pj][:, 512:640].rearrange("p (c k) -> p c k", k=32),
                                                     func=AF.Copy), [("pp", pj)], [("SU", c) for c in range(q * 4, q * 4 + 4)])

            for s in range(NST):
                ncols = TS + 2 if s < 2 else 642
                tiles = []
                c0 = 0
                while c0 + 2 < ncols:
                    c1 = min(c0 + 512, ncols)
                    tiles.append((c0, c1))
                    c0 = c1 - 2
                for j in range(NJ):
                    k = s * NJ + j
                    issue_w(k + NW - 1)
                    pg.dma("pool", wdn[:, j, :], w_down[li, j * 128:(j + 1) * 128, :], [], [("wdn", j)], ("wdn", j))
                    ws = k % NW
                    if s + 1 < NST and 2 <= j <= 12 and j % 2 == 0:
                        stats_square(s + 1, 8 + 6 * (s + 1), j // 2 - 1)
                    if s + 1 < NST and j == 14:
                        stats_final(8 + 6 * (s + 1))
                    for (c0, c1) in tiles:
                        conv_tile(s, j, ws, c0, c1, s == 2 and c1 == ncols)
                    if s == 2:
                        sample_tile(j, ws)
                if s + 1 < NST:
                    prenorm(s + 1)
                else:
                    emit_ffn_rows()
                for (B, np_, b) in blocks(s):
                    pj = next_pp()
                    for j in range(NJ):
                        for half in range(2):
                            pg.add("pe", lambda e: e.matmul(
                                out=pp[pj][:np_, half * 512:(half + 1) * 512], lhsT=GT[:, j, b * 128:b * 128 + np_],
                                rhs=wdn[:, j, half * 512:(half + 1) * 512], start=(j == 0), stop=(j == NJ - 1)),
                                [("GT", j), ("wdn", j)], [("pp", pj)], sig=(j == NJ - 1 and half == 1))
                    postnorm_residual(pj, B, np_, 1)
                    if final:
                        store_x_block(s, B, np_, b)
    held = set()

    def alloc_pp():
        for _ in range(8):
            i = next_pp()
            if i not in held:
                held.add(i)
                return i
        raise RuntimeError("all PSUM pairs held")

    def hgrn_stage():
        with ExitStack() as ps:
            sbp = lambda name, shape, dt=F32: ps.enter_context(nc.sbuf_tensor("%s_%d" % (name, uid()), list(shape), dt))
            hT = sbp("hT", (128, 8, TS), BF16)
            oT = sbp("oT", (128, 8, TS), BF16)
            NWI = 3
            win = [sbp("win%d" % i, (128, 4, 8, 128), BF16) for i in range(NWI)]
            ZDS = sbp("ZDS", (128, 3 * 1536))
            Zd = ZDS[:, 0:1536].rearrange("p (e c) -> p e c", c=12)
            D0 = ZDS[:, 1536:3072].rearrange("p (e c) -> p e c", c=12)
            Sall = ZDS[:, 3072:4608].rearrange("p (e c) -> p e c", c=12)
            wout = ZDS.bitcast(BF16)[:, 0:8192].rearrange("p (h n) -> p h n", h=8)
            ZK = ["Zd", "D0", "Sall"]
            W1 = sbp("W1", (128, TS))
            W2 = sbp("W2", (128, TS))
            W3 = sbp("W3", (128, TS))
            W4 = sbp("W4", (128, TS))
            W5 = sbp("W5", (128, TS))
            qh = [sbp("qh%d" % i, (128, TS), BF16) for i in range(2)]
            kh = [sbp("kh%d" % i, (128, TS), BF16) for i in range(2)]
            sqb = junk[:, 0:TS]
            vt = sbp("vt", (128, 11, 128), BF16)
            khT = sbp("khT", (128, 11, 128), BF16)
            AT = sbp("AT", (128, 11, 64), BF16)
            Sbf = sbp("Sbf", (128, 12, 128), BF16)
            Scar = sbp("Scar", (128, 8, 128))
            S0q = [sbp("S0q%d" % i, (128, 4, 128)) for i in range(2)]
            S0b = [sbp("S0b%d" % i, (128, 4, 128), BF16) for i in range(4)]
            Vblk = sbp("Vblk", (128, 16, 128), BF16)
            tmpS = [sbp("tmpS%d" % i, (128, 4, 128)) for i in range(2)]
            s0rr = [0]
            tri = sbp("tri", (64, 64))
            blk = sbp("blk", (64, 64))
            seqm = sbp("seqm", (128, 16))
            rst = sbp("rst", (128, 2, TS), BF16)
            hbf = [sbp("hbf%d" % i, (128, D), BF16) for i in range(1)]
            ones_b = sbp("ones_b", (128, 128), BF16)
            lbl = sbp("lbl", (128, 8, 2))
            lbs = sbp("lbs", (128, 4, 8))
            gn = sbp("gn", (128, 2))
            ebl = [sbp("ebl%d" % i, (128, 12)) for i in range(2)]
            ebs = [sbp("ebs%d" % i, (128, 16)) for i in range(2)]
            lrow = T2
            load_gb(0, n_mpre, 1)
            load_gb(1, n_mpost, 1)
            pg.dma("sp", tri[:], c_tri, [], ["tri"], "hc0")
            pg.dma("sp", blk[:], c_blk, [], ["blk"], "hc1")
            pg.add("dve", lambda e: e.memset(seqm[:], 0.0), [], ["seqm"])
            pg.dma("sp", seqm[0:64, :], c_seqm, [], ["seqm"], "hc2")
            pg.add("dve", lambda e: e.memset(vt[:], 0.0), [], ["vt"])
            pg.add("dve", lambda e: e.memset(khT[:], 0.0), [], ["khT"])
            pg.add("dve", lambda e: e.memset(AT[:], 0.0), [], ["AT"])
            pg.dma("pool", rst[:], c_rst.rearrange("p (a t) -> p a t", a=2), [], ["rst"], "hc3")
            pg.dma("sp", lrow[0:2, :], lb_logits, [], T2K, "hc4")
            pg.dma("sp", gn[:, 0:1], bass.AP(tensor=gnorm.tensor, offset=0, ap=[[1, 128], [1, 1]]), [], ["gn"], "hc5")
            pg.add("dve", lambda e: e.memset(gn[:, 1:2], EPS), [], ["gn"])
            pg.add("dve", lambda e: e.memset(ones_b[:], 1.0), [], ["ones_b"])
            pg.add("dve", lambda e: e.memset(Scar[:], 0.0), [], ["Scar"])
            for i in range(2):
                pg.add("dve", lambda e: e.memset(ebl[i][:], 0.0), [], [("ebl", i)])
            pg.add("dve", lambda e: e.memset(ZDS[:], 0.0), [], ZK + ["wout"])
            pj = next_pp()
            for c in range(8):
                pg.add("pe", lambda e: e.transpose(out=pp[pj][:, c * 2:c * 2 + 2], in_=lrow[0:2, c * 128:(c + 1) * 128],
                                                   identity=ident_f[:2, :2]), T2K + ["ident_f"], [("pp", pj)], sig=(c == 7))
            pg.add("act", lambda e: e.activation(out=lbl[:], in_=pp[pj][:, 0:16].rearrange("p (c k) -> p c k", k=2), func=AF.Copy),
                   [("pp", pj)], ["lbl"])
            pg.add("dve", lambda e: e.tensor_tensor(out=lbs[:, 2, :], in0=lbl[:, :, 1], in1=lbl[:, :, 0], op=ALU.subtract), ["lbl"], ["lbs"])
            pg.add("act", lambda e: e.activation(out=lbs[:, 3, :], in_=lbs[:, 2, :], func=AF.Sigmoid), ["lbs"], ["lbs"])
            pg.add("dve", lambda e: e.tensor_scalar(out=lbs[:, 0, :], in0=lbs[:, 3, :], scalar1=-1.0, scalar2=None, op0=ALU.add), ["lbs"], ["lbs"])
            pg.add("dve", lambda e: e.tensor_scalar(out=lbs[:, 1, :], in0=lbs[:, 3, :], scalar1=-1.0, scalar2=1.0, op0=ALU.mult, op1=ALU.add),
                   ["lbs"], ["lbs"])
            SC = float(128 ** -0.5)

            def issue_win(g):
                if g >= NST * 8:
                    return
                h_ = g % 8
                ws_ = g % NWI
                for t2 in range(2):
                    pg.dma("pool", win[ws_][:, 2 * t2:2 * t2 + 2, :, :].rearrange("p a b c -> p (a b c)"),
                           w_in[h_, :, t2 * 2048:(t2 + 1) * 2048], [], [("win", ws_)], ("win", ws_, t2))

            def proj(ws, typ):
                pj = alloc_pp()
                for (a, b_) in ((0, 512), (512, TS)):
                    for kc in range(8):
                        pg.add("pe", lambda e: e.matmul(out=pp[pj][:, a:b_], lhsT=win[ws][:, typ, kc, :], rhs=hT[:, kc, a:b_],
                                                        start=(kc == 0), stop=(kc == 7)),
                               [("win", ws), "hT"], [("pp", pj)], sig=(kc == 7 and a == 512))
                return pj

            def stage_a(s, h, g):
                p = g % 2
                ws = g % NWI
                npc = 11 if s < 2 else 10
                ri = 0 if s < 2 else 1
                pj = proj(ws, 1)
                pk = ("pp", pj)
                yield
                pg.add("act", lambda e: e.activation(out=W1[:], in_=pp[pj][:, 0:TS], func=AF.Sigmoid, scale=-1.0), [pk], ["W1"])
                held.discard(pj)
                yield
                pg.add("dve", lambda e: e.tensor_scalar(out=W3[:], in0=W1[:], scalar1=lbs[:, 0, h:h + 1], scalar2=1.0,
                                                        op0=ALU.mult, op1=ALU.add), ["W1", "lbs"], ["W3"])
                yield
                pg.add("act", lambda e: e.activation(out=W3[:], in_=W3[:], func=AF.Ln), ["W3"], ["W3"])
                yield
                pg.add("dve", lambda e: e.tensor_tensor_scan(out=W2[:], data0=rst[:, ri, :], data1=W3[:], initial=0.0,
                                                             op0=ALU.mult, op1=ALU.add), ["W3", "rst"], ["W2"])
                yield
                pg.add("act", lambda e: e.activation(out=W3[:], in_=W2[:], func=AF.Exp, scale=-1.0), ["W2"], ["W3"])
                yield
                pg.add("dve", lambda e: e.scalar_tensor_tensor(out=kh[p][:], in0=W1[:], scalar=lbs[:, 1, h:h + 1], in1=W3[:],
                                                               op0=ALU.mult, op1=ALU.mult), ["W1", "W3", "lbs"], [("kh", p)])
                yield
                pg.add("act", lambda e: e.activation(out=W1[:], in_=W2[:], func=AF.Exp), ["W2"], ["W1"])
                yield
                pg.add("dve", lambda e: e.tensor_copy(out=ebl[p][:, 1:1 + npc],
                                                      in_=W1[:, 0:npc * 64].rearrange("p (c t) -> p c t", t=64)[:, :, 63]),
                       ["W1"], [("ebl", p)])
                if s == 2:
                    pg.add("dve", lambda e: e.tensor_copy(out=ebs[p][:],
                                                          in_=W1[:, 640:704].rearrange("p (c t) -> p c t", t=4)[:, :, 3]),
                           ["W1"], [("ebs", p)])
                yield
                pj = proj(ws, 0)
                pk = ("pp", pj)
                yield
                pg.add("act", lambda e: e.activation(out=W2[:], in_=pp[pj][:, 0:TS], func=AF.Silu), [pk], ["W2"])
                held.discard(pj)
                yield
                pg.add("dve", lambda e: e.scalar_tensor_tensor(out=qh[p][:], in0=W2[:], scalar=SC, in1=W1[:],
                                                               op0=ALU.mult, op1=ALU.mult), ["W2", "W1"], [("qh", p)])
                yield

            s0_pref = {}
            sb_pref = {}
            sbrr = [0]

            def stage_b(s, h, g):
                p = g % 2
                ws = g % NWI
                nch = 11
                npc = 11 if s < 2 else 10
                khp, qhp = kh[p], qh[p]
                kk, qk = ("kh", p), ("qh", p)
                for (ca, cb) in ((0, 8), (8, 11)):
                    pj = alloc_pp()
                    for c in range(ca, cb):
                        for kc in range(8):
                            pg.add("pe", lambda e: e.matmul(out=pp[pj][0:64, (c - ca) * 128:(c - ca + 1) * 128],
                                                            lhsT=hT[:, kc, c * 64:(c + 1) * 64], rhs=win[ws][:, 2, kc, :],
                                                            start=(kc == 0), stop=(kc == 7)),
                                   [("win", ws), "hT"], [("pp", pj)], sig=(kc == 7 and c == cb - 1))
                    yield
                    pg.add("act", lambda e: e.activation(out=vt[0:64, ca:cb, :],
                                                         in_=pp[pj][0:64, 0:(cb - ca) * 128].rearrange("p (c e) -> p c e", e=128),
                                                         func=AF.Copy), [("pp", pj)], ["vt"])
                    held.discard(pj)
                    yield
                pj = alloc_pp()
                for c in range(nch):
                    pg.add("pe", lambda e: e.transpose(out=ppb[pj][0:64, c * 128:(c + 1) * 128], in_=khp[:, c * 64:(c + 1) * 64],
                                                       identity=ident_b[:, :]), [kk, "ident_b"], [("pp", pj)], sig=(c == nch - 1))
                yield
                pg.add("dve", lambda e: e.tensor_copy(out=khT[0:64, :, :], in_=ppb[pj][0:64, 0:nch * 128].rearrange("p (c e) -> p c e", e=128)),
                       [("pp", pj)], ["khT"])
                held.discard(pj)
                yield
                pj = alloc_pp()
                for c in range(nch):
                    pg.add("pe", lambda e: e.matmul(out=pp[pj][0:64, c * 64:(c + 1) * 64], lhsT=khp[:, c * 64:(c + 1) * 64],
                                                    rhs=qhp[:, c * 64:(c + 1) * 64], start=True, stop=True),
                           [kk, qk], [("pp", pj)], sig=(c == nch - 1))
                yield
                pg.add("dve", lambda e: e.tensor_tensor(
                    out=AT[0:64, 0:npc, :], in0=pp[pj][0:64, 0:npc * 64].rearrange("p (c t) -> p c t", t=64),
                    in1=bcast(tri[:, :].rearrange("p (o t) -> p o t", o=1), 1, npc), op=ALU.mult),
                    [("pp", pj), "tri"], ["AT"])
                if s == 2:
                    pg.add("dve", lambda e: e.tensor_tensor(out=AT[0:64, 10, :], in0=pp[pj][0:64, 640:704], in1=blk[:, :], op=ALU.mult),
                           [("pp", pj), "blk"], ["AT"])
                held.discard(pj)
                yield
                pg.add("act", lambda e: e.activation(out=Zd[:, :, 0], in_=Scar[:, h, :], func=AF.Copy), ["Scar"], ["Zd"])
                for (ca, cb) in ((0, 8), (8, npc)):
                    pj = alloc_pp()
                    for c in range(ca, cb):
                        pg.add("pe", lambda e: e.matmul(out=pp[pj][:, (c - ca) * 128:(c - ca + 1) * 128], lhsT=khT[:, c, :],
                                                        rhs=vt[:, c, :], start=True, stop=True),
                               ["khT", "vt"], [("pp", pj)], sig=(c == cb - 1))
                    yield
                    pg.add("dve", lambda e: e.tensor_tensor(
                        out=Zd[:, :, 1 + ca:1 + cb].rearrange("p e c -> p c e"),
                        in0=pp[pj][:, 0:(cb - ca) * 128].rearrange("p (c e) -> p c e", e=128),
                        in1=bcast(ebl[p][:, 1 + ca:1 + cb].rearrange("p (c o) -> p c o", o=1), 2, 128), op=ALU.mult),
                        [("pp", pj), ("ebl", p)], ["Zd"])
                    held.discard(pj)
                    yield
                pjg = proj(ws, 3)
                yield
                pg.add("dve", lambda e: e.tensor_copy(
                    out=D0[:, :, :], in_=bcast(ebl[p][:, 0:12].rearrange("p (o c) -> p o c", o=1), 1, 128)),
                    [("ebl", p)], ["D0"])
                yield
                pg.add("dve", lambda e: e.tensor_tensor_scan(
                    out=Sall.rearrange("p e c -> p (e c)"), data0=D0.rearrange("p e c -> p (e c)"),
                    data1=Zd.rearrange("p e c -> p (e c)"), initial=0.0, op0=ALU.mult, op1=ALU.add),
                    ["D0", "Zd"], ["Sall"])
                yield
                pg.add("dve", lambda e: e.tensor_copy(out=Sbf[:, 0:6, :], in_=Sall[:, :, 0:6].rearrange("p e c -> p c e")),
                       ["Sall"], [("Sbf", 0)])
                pg.add("act", lambda e: e.activation(out=Sbf[:, 6:npc, :], in_=Sall[:, :, 6:npc].rearrange("p e c -> p c e"),
                                                     func=AF.Copy), ["Sall"], [("Sbf", 1)])
                pg.add("act", lambda e: e.activation(out=Scar[:, h, :], in_=Sall[:, :, npc], func=AF.Copy), ["Sall"], ["Scar"])
                if s == 2:
                    pg.dma("sp", o_hgrn_p[h], Scar[:, h, :], ["Scar"], [], "ohp")
                yield
                po = alloc_pp()
                pok = ("pp", po)
                for c in range(npc):
                    pg.add("pe", lambda e: e.matmul(out=pp[po][:, c * 64:(c + 1) * 64], lhsT=vt[:, c, :], rhs=AT[:, c, :],
                                                    start=True, stop=False), ["vt", "AT"], [pok], sig=False)
                    pg.add("pe", lambda e: e.matmul(out=pp[po][:, c * 64:(c + 1) * 64], lhsT=Sbf[:, c, :], rhs=qhp[:, c * 64:(c + 1) * 64],
                                                    start=False, stop=True), [("Sbf", 0 if c < 6 else 1), qk], [pok], sig=(c == npc - 1))
                if s == 2:
                    pg.add("pe", lambda e: e.matmul(out=pp[po][:, 640:704], lhsT=vt[:, 10, :], rhs=AT[:, 10, :],
                                                    start=True, stop=False), ["vt", "AT"], [pok], sig=False)
                    def sb_load(hh, qg_):
                        pg.dma("pool", S0b[qg_][:], st_hgrn[4 * qg_:4 * qg_ + 4, hh].rearrange("s d e -> d s e"), [], [("S0b", qg_)], ("s0b", qg_))
                    if h == 0:
                        for qg in range(4):
                            sb_load(0, qg)
                    for qg in range(4):
                        i_ = qg
                        for sq_ in range(4):
                            sidx = 4 * qg + sq_
                            last = (qg == 3 and sq_ == 3)
                            pg.add("pe", lambda e: e.matmul(out=pp[po][:, 640 + 4 * sidx:644 + 4 * sidx], lhsT=S0b[i_][:, sq_, :],
                                                            rhs=qhp[:, 640 + 4 * sidx:644 + 4 * sidx], start=False, stop=last),
                                   [("S0b", i_), qk], [pok], sig=(sq_ == 3))
                        if h + 1 < 8:
                            sb_load(h + 1, qg)
                    yield
                yield
                pg.add("act", lambda e: e.activation(out=sqb, in_=pp[po][:, 0:TS], func=AF.Square), [pok], ["junk"])
                yield
                pn = alloc_pp()
                for (a, b_) in ((0, 512), (512, TS)):
                    pg.add("pe", lambda e: e.matmul(out=pp[pn][:, a:b_], lhsT=ones_b[:, :], rhs=sqb[:, a:b_], start=True, stop=True),
                           ["junk", "ones_b"], [("pp", pn)], sig=(a == 512))
                yield
                pg.add("act", lambda e: e.activation(out=W4[:], in_=pp[pn][:, 0:TS], func=AF.Ln, scale=1.0 / 128, bias=gn[:, 1:2]),
                       [("pp", pn), "gn"], ["W4"])
                held.discard(pn)
                yield
                pg.add("act", lambda e: e.activation(out=W4[:], in_=W4[:], func=AF.Exp, scale=-0.5), ["W4"], ["W4"])
                yield
                pg.add("dve", lambda e: e.scalar_tensor_tensor(out=W5[:], in0=pp[po][:, 0:TS], scalar=gn[:, 0:1], in1=W4[:],
                                                               op0=ALU.mult, op1=ALU.mult), [pok, "W4", "gn"], ["W5"])
                held.discard(po)
                yield
                pg.add("act", lambda e: e.activation(out=W4[:], in_=pp[pjg][:, 0:TS], func=AF.Silu), [("pp", pjg)], ["W4"])
                held.discard(pjg)
                yield
                pg.add("dve", lambda e: e.tensor_tensor(out=oT[:, h, :], in0=W5[:], in1=W4[:], op=ALU.mult), ["W5", "W4"], ["oT"])
                yield
                if s == 2:
                    def sq_load(hh, qg_):
                        i_ = s0rr[0] % 2
                        s0rr[0] += 1
                        pg.dma("sp", S0q[i_][:], st_hgrn[4 * qg_:4 * qg_ + 4, hh].rearrange("s d e -> d s e"), [], [("S0q", i_)], ("s0q", i_))
                        return i_
                    pg.add("dve", lambda e: e.tensor_tensor(
                        out=Vblk[:], in0=bcast(vt[:, 10:11, :], 1, 16),
                        in1=bcast(seqm[:, 0:16].rearrange("p (c o) -> p c o", o=1), 2, 128), op=ALU.mult),
                        ["vt", "seqm"], ["Vblk"])
                    nxt = s0_pref.pop(h, None)
                    if nxt is None:
                        nxt = sq_load(h, 0)
                    for qg in range(4):
                        i_ = nxt
                        if qg < 3:
                            nxt = sq_load(h, qg + 1)
                        elif h + 1 < 8:
                            s0_pref[h + 1] = sq_load(h + 1, 0)
                        pz = alloc_pp()
                        pg.add("pe", lambda e: e.matmul(out=pp[pz][:, 0:512], lhsT=khT[:, 10, :],
                                                        rhs=Vblk[:, 4 * qg:4 * qg + 4, :].rearrange("p c e -> p (c e)"), start=True, stop=True),
                               ["khT", "Vblk"], [("pp", pz)])
                        pg.add("dve", lambda e: e.tensor_tensor(out=tmpS[i_][:], in0=pp[pz][:, 0:512].rearrange("p (c e) -> p c e", e=128),
                                                                in1=S0q[i_][:], op=ALU.add), [("pp", pz), ("S0q", i_)], [("tmpS", i_)])
                        held.discard(pz)
                        pg.add("dve", lambda e: e.tensor_tensor(
                            out=tmpS[i_][:], in0=tmpS[i_][:], in1=bcast(ebs[p][:, 4 * qg:4 * qg + 4].rearrange("p (c o) -> p c o", o=1), 2, 128),
                            op=ALU.mult), [("tmpS", i_), ("ebs", p)], [("tmpS", i_)])
                        pg.dma("sp", o_hgrn_s[4 * qg:4 * qg + 4, h].rearrange("s d e -> d s e"), tmpS[i_][:], [("tmpS", i_)], [], ("ohs", i_))
                        yield

            def run_gens(gens, lead=0):
                gens = [g_ for g_ in gens if g_ is not None]
                for _ in range(lead):
                    try:
                        next(gens[0])
                    except StopIteration:
                        gens.pop(0)
                        break
                while gens:
                    for g_ in list(gens):
                        try:
                            next(g_)
                        except StopIteration:
                            gens.remove(g_)

            def prenorm_s(s):
                if s == 0:
                    prenorm_stats(0, 8)
                for (B, np_, b) in blocks(s):
                    hi_ = hb_rr[0] % 2
                    hb_rr[0] += 1
                    pg.add("dve", lambda e: e.scalar_tensor_tensor(
                        out=hb2[hi_][:np_, :], in0=X[:np_, B, :], scalar=stat[:np_, 8 + 6 * s + b:9 + 6 * s + b], in1=gb[0][:np_, :],
                        op0=ALU.mult, op1=ALU.mult), [("X", B), ("stat", 8 + 6 * s), ("gb", 0)], [("T2h", hi_)])
                    pj = next_pp()
                    for kc in range(8):
                        pg.add("pe", lambda e: e.transpose(
                            out=ppb[pj][:, kc * 128:kc * 128 + np_], in_=hb2[hi_][:np_, kc * 128:(kc + 1) * 128],
                            identity=ident_b[:np_, :np_]), [("T2h", hi_), "ident_b"], [("pp", pj)], sig=(kc == 7))
                    pg.add("act", lambda e: e.activation(
                        out=hT[:, :, b * 128:b * 128 + np_],
                        in_=ppb[pj][:, 0:1024].rearrange("p (c t) -> p c t", c=8)[:, :, 0:np_], func=AF.Copy),
                        [("pp", pj)], ["hT"])

            def outproj_gen(s):
                pg.dma("pool", wout, w_out.rearrange("(h e) n -> e h n", e=128), [], ZK + ["wout"], "wout")
                yield
                for (B, np_, b) in blocks(s):
                    pj = alloc_pp()
                    for h in range(8):
                        for half in range(2):
                            pg.add("pe", lambda e: e.matmul(out=pp[pj][:np_, half * 512:(half + 1) * 512], lhsT=oT[:, h, b * 128:b * 128 + np_],
                                                            rhs=wout[:, h, half * 512:(half + 1) * 512], start=(h == 0), stop=(h == 7)),
                                   ["oT", "wout"] + ZK, [("pp", pj)], sig=(h == 7 and half == 1))
                    yield
                    postnorm_residual(pj, B, np_, 1)
                    held.discard(pj)
                    yield

            for g0 in range(NWI):
                issue_win(g0)
            prenorm_s(0)
            run_gens([stage_a(0, 0, 0)])
            for s in range(NST):
                for h in range(8):
                    g = s * 8 + h
                    ga = stage_a(s, h + 1, g + 1) if h + 1 < 8 else None
                    if s + 1 < NST and h < 6:
                        stats_square(s + 1, 8 + 6 * (s + 1), h)
                    if s + 1 < NST and h == 6:
                        stats_final(8 + 6 * (s + 1))
                    run_gens([stage_b(s, h, g), ga], lead=HG_LEAD)
                    issue_win(g + NWI)
                if s + 1 < NST:
                    prenorm_s(s + 1)
                ga = stage_a(s + 1, 0, (s + 1) * 8) if s + 1 < NST else None
                run_gens([outproj_gen(s), ga])

    if "pool" in STAGES:
        pool_stage()
        pg.barrier()
    else:
        load_x()
    if "ffn0" in STAGES:
        ffn_stage(0)
        pg.barrier()
    if "hgrn" in STAGES:
        hgrn_stage()
        pg.barrier()
    if "ffn1" in STAGES:
        ffn_stage(1, final=True)
        pg.barrier()

    if "ffn1" not in STAGES:
        for s in range(NST):
            dst = yp[s * TS:s * TS + 640, :].rearrange("(b p) d -> p b d", p=128)
            pg.dma("sp", dst, X[:, s * 6:s * 6 + 5, :], [("X", s * 6 + b) for b in range(5)], [], ("xs", s))
            if s < 2:
                pg.dma("sp", yp[s * TS + 640:s * TS + 704, :], X[0:64, s * 6 + 5, :], [("X", s * 6 + 5)], [], ("xs5", s))
            else:
                pg.dma("sp", ys, X[0:64, 17, :], [("X", 17)], [], ("xs5", s))

    pg.emit()
    es.close()
    return nc


_CONSTS = None


def _consts():
    global _CONSTS
    if _CONSTS is None:
        ident = np.eye(128, dtype=np.float32)
        ti = np.arange(128)[:, None]
        to = np.arange(128)[None, :]
        band = np.zeros((128, 16, 128), np.float32)
        bs = np.zeros((128, 4, 24), np.float32)
        for g in range(4):
            w = 2 ** (g + 1)
            d = to - ti
            band[:, g, :] = ((d >= 0) & (d < w)) / w - (d == 0)
            dp = to + 128 - ti
            band[:, 4 + g, :] = ((dp > 0) & (dp < w)) / w
            d64 = to + 64 - ti
            band[:, 8 + g, :] = (((d64 > 0) & (d64 < w)) / w) * (ti < 64)
            cnt = np.minimum(to + 1, w)
            band[:, 12 + g, :] = ((d >= 0) & (d < w)) / cnt - (d == 0)
            for q in range(6):
                for r in range(19):
                    for t in range(4):
                        dd = 15 + t - r
                        bs[q * 19 + r, g, q * 4 + t] = (1.0 / w if 0 <= dd < w else 0.0) - (1.0 if dd == 0 else 0.0)
        tri = np.triu(np.ones((64, 64), np.float32))
        sid = np.arange(64) // 4
        blk = tri * (sid[:, None] == sid[None, :])
        seqm = (sid[:, None] == np.arange(16)[None, :]).astype(np.float32)
        rst = np.ones((128, 2, TS), np.float32)
        rst[:, :, ::64] = 0.0
        rst[:, 1, 640::4] = 0.0
        rst = rst.reshape(128, 2 * TS)
        _CONSTS = dict(c_ident=ident, c_band=band.reshape(128, 2048), c_bs=bs.reshape(128, 96), c_tri=tri, c_blk=blk.astype(np.float32),
                       c_seqm=seqm, c_rst=rst)
    return _CONSTS


_NC = None


def kernel(x_prompt, x_sample, state_pool, state_hgrn, state_ffn_conv, norm_mix_pre, norm_mix_post,
           norm_ffn_pre, norm_ffn_post, pool_w, pool_scale, hgrn_w_in, hgrn_lb_logits, hgrn_gnorm,
           hgrn_w_out, ffn_w_up, ffn_conv_w, ffn_conv_b, ffn_w_down):
    global _NC
    f = lambda a: np.ascontiguousarray(np.asarray(a, dtype=np.float32))
    x_prompt, x_sample, state_pool, state_hgrn, state_ffn_conv = map(f, (x_prompt, x_sample, state_pool, state_hgrn, state_ffn_conv))
    if _NC is None:
        _NC = build_program()
    nc = _NC
    shared = dict(
        n_mpre=f(norm_mix_pre), n_mpost=f(norm_mix_post), n_fpre=f(norm_ffn_pre), n_fpost=f(norm_ffn_post),
        pool_w=f(pool_w)[0], pool_scale=f(pool_scale), lb_logits=f(hgrn_lb_logits),
        w_in=np.ascontiguousarray(f(hgrn_w_in)[0].reshape(8, 128, 4, 8, 128).transpose(3, 1, 2, 0, 4)).reshape(8, 128, 4096),
        w_up=np.ascontiguousarray(f(ffn_w_up).reshape(2, 8, 128, 2, NJ, 128).transpose(0, 4, 2, 1, 3, 5)).reshape(2, NJ, 128, 2048),
        gnorm=f(hgrn_gnorm), w_out=f(hgrn_w_out)[0], conv_w=f(ffn_conv_w).reshape(6, 2 * DFF),
        conv_b=f(ffn_conv_b), w_down=f(ffn_w_down), **_consts())
    in_maps = []
    for i in range(NCORES):
        sl = slice(i * NSS, (i + 1) * NSS)
        m = dict(shared)
        m["xp"] = x_prompt[i]
        m["xs"] = x_sample[sl].reshape(64, D)
        m["st_pool"] = state_pool[0, sl]
        m["st_hgrn"] = state_hgrn[0, sl]
        m["st_ffn"] = state_ffn_conv[:, sl].reshape(2, NSS * 2, 2 * DFF)
        in_maps.append(m)
    res = run_bass_kernel_spmd(nc, in_maps, core_ids=list(range(NCORES)))
    R = res.results
    cat = lambda k: np.stack([np.asarray(r[k]) for r in R], 0)
    y_prompt = cat("yp")
    y_sample = cat("ys").reshape(128, 4, D)
    pool_p = cat("o_pool_p")[None]
    pool_s = cat("o_pool_s").reshape(1, 128, 15, D)
    hgrn_p = cat("o_hgrn_p")[None]
    hgrn_s = cat("o_hgrn_s").reshape(1, 128, 8, 128, 128)
    ffn_p = cat("o_ffn_p").transpose(1, 0, 2, 3)
    ffn_s = cat("o_ffn_s").reshape(8, 2, NSS, 2, 2 * DFF).transpose(1, 0, 2, 3, 4).reshape(2, 128, 2, 2 * DFF)
    return (y_prompt, y_sample, pool_p, pool_s, hgrn_p, hgrn_s, ffn_p, ffn_s)
```
